# Optimizing a Trainium2 kernel written in Bass

```python
import math
import jax, jax.numpy as jnp
from jax import lax
import numpy as np

D_MODEL = 1024
BATCH = 8
SEQ = 2048
DEPTH = 1
DEC_BATCH = 16
DEC_SEQ = 2048
PAST_LEN = 128

N_HEADS = 8
QK_NOPE_DIM = 128
QK_ROPE_DIM = 64
QK_HEAD_DIM = QK_NOPE_DIM + QK_ROPE_DIM
V_HEAD_DIM = 128
Q_LORA_RANK = 384
KV_LORA_RANK = 256
ATTN_WIDTH = N_HEADS * V_HEAD_DIM
CONV_WIDTH = D_MODEL
CONV_KERNEL = 31
CONV_PAD = (CONV_KERNEL - 1) // 2
ROPE_THETA = 10000.0
Q_BLOCK = 128
EPS = 1e-6
SPLIT_SIZES = (Q_LORA_RANK, KV_LORA_RANK, QK_ROPE_DIM, ATTN_WIDTH, 2 * CONV_WIDTH, CONV_WIDTH, D_MODEL, D_MODEL)
D_IN = Q_LORA_RANK + KV_LORA_RANK + QK_ROPE_DIM + ATTN_WIDTH + 2 * CONV_WIDTH + CONV_WIDTH + D_MODEL + D_MODEL

kernel_name = "hybrid_mla_conformer_gated_encoder"


def rms_norm(x, g):
    xf = x.astype(jnp.float32)
    y = xf * lax.rsqrt(jnp.mean(xf * xf, axis=-1, keepdims=True) + EPS) * g.astype(jnp.float32)
    return y.astype(x.dtype)


def layer_norm(x, g, b):
    xf = x.astype(jnp.float32)
    mu = jnp.mean(xf, axis=-1, keepdims=True)
    var = jnp.mean(jnp.square(xf - mu), axis=-1, keepdims=True)
    y = (xf - mu) * lax.rsqrt(var + EPS) * g.astype(jnp.float32) + b.astype(jnp.float32)
    return y.astype(x.dtype)


def split_columns(p):
    outs = []
    off = 0
    for n in SPLIT_SIZES:
        outs.append(p[..., off:off + n])
        off += n
    return outs


def rope_tables(seq_len, dtype):
    half = QK_ROPE_DIM // 2
    inv_freq = 1.0 / (ROPE_THETA ** (jnp.arange(half, dtype=jnp.float32) / half))
    ang = jnp.arange(seq_len, dtype=jnp.float32)[:, None] * inv_freq[None, :]
    return jnp.cos(ang).astype(dtype)[None, :, None, :], jnp.sin(ang).astype(dtype)[None, :, None, :]


def apply_rope(x, cos, sin):
    x1, x2 = x[..., :QK_ROPE_DIM // 2], x[..., QK_ROPE_DIM // 2:]
    return jnp.concatenate([x1 * cos - x2 * sin, x2 * cos + x1 * sin], axis=-1)


def blocked_attention(q, k, v):
    B, S, H, dh = q.shape
    nb = S // Q_BLOCK
    scale = 1.0 / math.sqrt(QK_HEAD_DIM)
    qb = jnp.transpose(q.reshape(B, nb, Q_BLOCK, H, dh), (1, 0, 2, 3, 4))

    def one_block(qi):
        s = jnp.einsum('bqhd,bkhd->bhqk', qi, k).astype(jnp.float32) * scale
        p = jax.nn.softmax(s, axis=-1).astype(v.dtype)
        return jnp.einsum('bhqk,bkhd->bqhd', p, v)

    o = lax.map(one_block, qb)
    return jnp.transpose(o, (1, 0, 2, 3, 4)).reshape(B, S, H * V_HEAD_DIM)


def hybrid_layer(x, norm_g, w_in, q_lora_g, w_uq, kv_lora_g, w_ukv, q_head_g, k_head_g, w_o_attn,
                 dw_kernel, dw_bias, conv_ln_g, conv_ln_b, w_pw2, w_out):
    B, S, _ = x.shape
    h = rms_norm(x, norm_g)
    proj = jnp.einsum('bsd,de->bse', h, w_in)
    c_q, c_kv, k_rope, gate_a, conv_in, gate_c, mg_a, mg_c = split_columns(proj)

    q = jnp.einsum('bsr,re->bse', rms_norm(c_q, q_lora_g), w_uq).reshape(B, S, N_HEADS, QK_HEAD_DIM)
    kv = jnp.einsum('bsr,re->bse', rms_norm(c_kv, kv_lora_g), w_ukv).reshape(B, S, N_HEADS, QK_NOPE_DIM + V_HEAD_DIM)
    k_nope, v = kv[..., :QK_NOPE_DIM], kv[..., QK_NOPE_DIM:]
    k_rope_h = jnp.broadcast_to(k_rope[:, :, None, :], (B, S, N_HEADS, QK_ROPE_DIM))
    k = jnp.concatenate([k_nope, k_rope_h], axis=-1)
    q = rms_norm(q, q_head_g)
    k = rms_norm(k, k_head_g)
    cos, sin = rope_tables(S, x.dtype)
    q = jnp.concatenate([q[..., :QK_NOPE_DIM], apply_rope(q[..., QK_NOPE_DIM:], cos, sin)], axis=-1)
    k = jnp.concatenate([k[..., :QK_NOPE_DIM], apply_rope(k[..., QK_NOPE_DIM:], cos, sin)], axis=-1)
    attn = blocked_attention(q, k, v)
    y_a = jnp.einsum('bse,ed->bsd', attn * jax.nn.silu(gate_a), w_o_attn)

    u = conv_in[..., :CONV_WIDTH] * jax.nn.sigmoid(conv_in[..., CONV_WIDTH:])
    kern = dw_kernel.astype(u.dtype)[:, None, :]
    u = lax.conv_general_dilated(u, kern, window_strides=(1,), padding=[(CONV_PAD, CONV_PAD)],
                                 dimension_numbers=('NWC', 'WIO', 'NWC'),
                                 feature_group_count=CONV_WIDTH) + dw_bias
    u = jax.nn.silu(layer_norm(u, conv_ln_g, conv_ln_b)) * jax.nn.silu(gate_c)
    y_c = jnp.einsum('bsc,cd->bsd', u, w_pw2)

    merged = jax.nn.sigmoid(mg_a) * y_a + jax.nn.sigmoid(mg_c) * y_c
    return x + jnp.einsum('bsd,de->bse', merged, w_out)


def setup_inputs(seed: int = 0) -> dict:
    key = jax.random.key(seed)
    ks = jax.random.split(key, 20)
    f32 = jnp.float32

    def w(k, shape, fan_in):
        return jax.random.normal(k, shape, f32) * (fan_in ** -0.5)

    def gain(k, shape):
        return 1.0 + 0.02 * jax.random.normal(k, shape, f32)

    return {
        "x_prompt": jax.random.normal(ks[0], (BATCH, SEQ, D_MODEL), f32),
        "x_sample": jax.random.normal(ks[1], (DEC_BATCH, DEC_SEQ, D_MODEL), f32),
        "norm_g": gain(ks[2], (DEPTH, D_MODEL)),
        "w_in": w(ks[3], (DEPTH, D_MODEL, D_IN), D_MODEL),
        "q_lora_g": gain(ks[4], (DEPTH, Q_LORA_RANK)),
        "w_uq": w(ks[5], (DEPTH, Q_LORA_RANK, N_HEADS * QK_HEAD_DIM), Q_LORA_RANK),
        "kv_lora_g": gain(ks[6], (DEPTH, KV_LORA_RANK)),
        "w_ukv": w(ks[7], (DEPTH, KV_LORA_RANK, N_HEADS * (QK_NOPE_DIM + V_HEAD_DIM)), KV_LORA_RANK),
        "q_head_g": gain(ks[8], (DEPTH, QK_HEAD_DIM)),
        "k_head_g": gain(ks[9], (DEPTH, QK_HEAD_DIM)),
        "w_o_attn": w(ks[10], (DEPTH, ATTN_WIDTH, D_MODEL), ATTN_WIDTH),
        "dw_kernel": w(ks[11], (DEPTH, CONV_KERNEL, CONV_WIDTH), CONV_KERNEL),
        "dw_bias": 0.02 * jax.random.normal(ks[12], (DEPTH, CONV_WIDTH), f32),
        "conv_ln_g": gain(ks[13], (DEPTH, CONV_WIDTH)),
        "conv_ln_b": 0.02 * jax.random.normal(ks[14], (DEPTH, CONV_WIDTH), f32),
        "w_pw2": w(ks[15], (DEPTH, CONV_WIDTH, D_MODEL), CONV_WIDTH),
        "w_out": w(ks[16], (DEPTH, D_MODEL, D_MODEL), D_MODEL),
    }


def reference(x_prompt, x_sample, norm_g, w_in, q_lora_g, w_uq, kv_lora_g, w_ukv, q_head_g, k_head_g,
              w_o_attn, dw_kernel, dw_bias, conv_ln_g, conv_ln_b, w_pw2, w_out):
    y_prompt = x_prompt
    y_sample = x_sample
    for l in range(DEPTH):
        layer_args = (norm_g[l], w_in[l], q_lora_g[l], w_uq[l], kv_lora_g[l], w_ukv[l], q_head_g[l], k_head_g[l],
                      w_o_attn[l], dw_kernel[l], dw_bias[l], conv_ln_g[l], conv_ln_b[l], w_pw2[l], w_out[l])
        y_prompt = hybrid_layer(y_prompt, *layer_args)
        y_sample = hybrid_layer(y_sample, *layer_args)
    return (y_prompt, y_sample)
```

```python
import math
import numpy as np
import concourse.bass as bass
import concourse.mybir as mybir
from concourse.bass_utils import run_bass_kernel_spmd

F32 = mybir.dt.float32
BF16 = mybir.dt.bfloat16
ALU = mybir.AluOpType
AF = mybir.ActivationFunctionType

D = 1024
NH = 8
DH = 192
QR = 384
KVR = 256
DIN = 6848
O_CQ, O_CKV, O_KR, O_GA, O_CA, O_CB, O_GC, O_MA, O_MC = 0, 384, 640, 704, 1728, 2752, 3776, 4800, 5824
EPS = 1e-6
NCORES = 8
KW = 31


class Res:
    __slots__ = ("name", "w", "r", "dsem", "dcnt")

    def __init__(self, name):
        self.name = name
        self.w = None
        self.r = {}
        self.dsem = None
        self.dcnt = 0


class Sched:
    ENG = ("pe", "act", "dve", "pool", "sp")

    def __init__(self, nc):
        self.nc = nc
        self.sems = {e: nc.alloc_semaphore("tl_" + e) for e in self.ENG}
        self.cnt = {e: 0 for e in self.ENG}
        self.seen = {e: {} for e in self.ENG}
        self.prog = {e: [] for e in self.ENG}

    def _collect(self, eng, reads, writes):
        need = {}

        def add(ev, raw):
            if ev is None:
                return
            k, v, clk = ev
            if k not in need or need[k][0] < v:
                need[k] = (v, clk)

        for r in reads:
            add(r.w, True)
        for w in writes:
            add(w.w, False)
            for ev in w.r.values():
                add(ev, False)
        seen = self.seen[eng]
        wl = []
        for k, (v, clk) in need.items():
            if k == eng and eng == "pe":
                continue
            if seen.get(k, 0) >= v:
                continue
            wl.append((self.sems[k], v))
            seen[k] = v
            for k2, v2 in clk.items():
                if seen.get(k2, 0) < v2:
                    seen[k2] = v2
        return wl

    def op(self, eng, fn, reads=(), writes=()):
        wl = self._collect(eng, reads, writes)
        self.cnt[eng] += 1
        n = self.cnt[eng]
        semh = self.sems[eng]

        def emit(e):
            for s, v in wl:
                e.wait_ge(s, v)
            fn(e).then_inc(semh, 1)

        self.prog[eng].append(emit)
        clk = dict(self.seen[eng])
        clk[eng] = n
        ev = (eng, n, clk)
        for r in reads:
            r.r[eng] = ev
        for w in writes:
            w.w = ev
            w.r = {}

    def dma(self, eng, out, in_, semres, reads=(), writes=(), **kw):
        wl = self._collect(eng, reads, writes)
        R = semres
        if R.dsem is None:
            R.dsem = "d:" + R.name
            self.sems[R.dsem] = self.nc.alloc_semaphore("d_" + R.name)
        R.dcnt += 16
        v = R.dcnt
        semh = self.sems[R.dsem]

        def emit(e):
            for s, vv in wl:
                e.wait_ge(s, vv)
            e.dma_start(out=out, in_=in_, **kw).then_inc(semh, 16)

        self.prog[eng].append(emit)
        ev = (R.dsem, v, dict(self.seen[eng]))
        for r in reads:
            r.r[R.dsem] = ev
        for w in writes:
            w.w = ev
            w.r = {}

    def final_wait(self, eng, ress):
        wl = self._collect(eng, (), ress)

        def emit(e):
            for s, v in wl:
                e.wait_ge(s, v)

        self.prog[eng].append(emit)


def build_nc(S, NSEQ, dbg_stop=None):
    NT = S // 128
    NG = S // 512
    nc = bass.Bass("TRN2", target_bir_lowering=False)
    sch = Sched(nc)

    def din(name, shape):
        return nc.dram_tensor(name, list(shape), F32, kind="ExternalInput").ap()

    x = din("x", [NSEQ, S, D])
    y = nc.dram_tensor("y", [NSEQ, S, D], F32, kind="ExternalOutput").ap()
    w_in = din("w_in", [D, DIN])
    w_uq = din("w_uq", [QR, NH * DH])
    w_ukv = din("w_ukv", [KVR, NH * 256])
    w_o = din("w_o", [D, D])
    w_pw2 = din("w_pw2", [D, D])
    w_out = din("w_out", [D, D])
    d_ident = din("ident", [128, 128])
    d_normg = din("norm_g", [1, D])
    d_qlg = din("q_lora_g", [1, QR])
    d_kvlg = din("kv_lora_g", [1, KVR])
    d_gq = din("q_head_g", [1, DH])
    d_gk = din("k_head_g", [1, DH])
    d_cs = din("csss", [S, 128])
    d_dwk = din("dwk", [128, 8 * KW])
    d_vecs = din("vecs", [128, 24])
    d_diag = nc.dram_tensor("diag_scr", [8, 128, KW * 128], BF16, kind="Internal").ap()
    R_dscr = [Res(f"dscr{c}") for c in range(8)]

    def sb(name, shape, dt):
        return nc.alloc_sbuf_tensor("s_" + name, list(shape), dt)

    ident = sb("ident", [128, 128], BF16)
    identf = sb("identf", [128, 128], F32)
    ones_b = sb("ones_b", [128, 128], BF16)
    onesN = sb("onesN", [128, 128], BF16)
    gbc = sb("gbc", [128, D], F32)
    qlg = sb("qlg", [128, QR], F32)
    kvlg = sb("kvlg", [128, KVR], F32)
    gq = sb("gq", [128, DH], F32)
    gk = sb("gk", [128, DH], F32)
    gprod = sb("gprod", [128, 128], F32)
    NCS = 4
    csts = [sb(f"cst{i}", [128, 128], F32) for i in range(NCS)]
    R_cst = [Res(f"cst{i}") for i in range(NCS)]
    cctr = [0]

    def load_cs(t):
        i = cctr[0] % NCS
        cctr[0] += 1
        sch.dma("sp", csts[i][:, :], d_cs[t * 128:(t + 1) * 128, :], R_cst[i], writes=[R_cst[i]])
        return csts[i], R_cst[i]

    dwk = sb("dwk", [128, 8 * KW], F32)
    vecs = sb("vecs", [128, 24], F32)
    epst = sb("epst", [128, 1], F32)
    lnc = sb("lnc", [128, 1], F32)
    negB = sb("negB", [128, 4], F32)
    R_const = Res("const")

    KT = sb("KT", [128, NH, S], BF16)
    krTa = sb("krTa", [128, S], BF16)
    krTb = sb("krTb", [128, S], BF16)
    V = sb("V", [128, NT, D], BF16)
    sck = sb("sck", [128, NT, NH], F32)
    R_KT = [Res(f"KT{t}") for t in range(NT)]

    hT = sb("hT", [128, 8, 512], BF16)
    hTh = sb("hTh", [128, 8, 32], BF16)
    R_hT = [Res(f"hT{i}") for i in range(4)]
    R_hTh = Res("hTh")
    xts = [sb(f"xt{i}", [128, D], F32) for i in range(2)]
    R_xt = [Res(f"xt{i}") for i in range(2)]
    hbs = [sb(f"hb{i}", [128, D], BF16) for i in range(2)]
    R_hb = [Res(f"hb{i}") for i in range(2)]
    UQ = sb("UQ", [128, 8 * 544 + 8 * 512], BF16)
    u = UQ[:, 0:8 * 544].rearrange("p (c n) -> p c n", c=8)
    cv = UQ[:, 8 * 544:8 * 544 + 8 * 512].rearrange("p (c n) -> p c n", c=8)
    qT = UQ[:, 0:12 * 512].rearrange("p (c n) -> p c n", c=12)
    R_u = [Res(f"u{c}") for c in range(8)]
    R_cv = [Res(f"cv{c}") for c in range(8)]
    R_qT = [Res(f"qT{i}") for i in range(4)]
    smaF = UQ[:, :].bitcast(F32)
    smc = UQ[:, 0:8 * 512].rearrange("p (c n) -> p c n", c=8)
    R_smc = [Res(f"smc{i}") for i in range(8)]
    R_sma = [Res(f"sma{i}") for i in range(8)]
    attn = sb("attn", [128, 8, 512], BF16)
    R_attn = [Res(f"attn{h}") for h in range(8)]
    mc = sb("mc", [128, 8, 512], BF16)
    R_mc = [Res(f"mc{c}") for c in range(8)]

    NSLOT = 8
    AHEAD = 5
    wslots = [sb(f"ws{i}", [128, 8 * 128], BF16) for i in range(NSLOT)]
    R_ws = [Res(f"ws{i}") for i in range(NSLOT)]
    wctr = [0]

    def wtile(src, col0, ncols, KC):
        i = wctr[0] % NSLOT
        wctr[0] += 1
        view = wslots[i][:, 0:KC * ncols].rearrange("p (k n) -> p k n", k=KC)
        sch.dma("pool", view, src[:, col0:col0 + ncols].rearrange("(k p) n -> p k n", p=128),
                R_ws[i], writes=[R_ws[i]])
        return view, R_ws[i]

    class WStream:
        def __init__(self, reqs):
            self.reqs = reqs
            self.issued = []
            self.pos = 0

        def prefetch(self):
            while len(self.issued) < min(len(self.reqs), self.pos + 1 + AHEAD):
                self.issued.append(wtile(*self.reqs[len(self.issued)]))

        def get(self, ahead=AHEAD):
            while len(self.issued) < min(len(self.reqs), self.pos + 1 + ahead):
                self.issued.append(wtile(*self.reqs[len(self.issued)]))
            r = self.issued[self.pos]
            self.pos += 1
            return r

    junk = sb("junk", [128, 512], BF16)
    R_junk = [Res(f"junk{i}") for i in range(4)]
    jctr = [0]

    def jslot(ncols, npart=128):
        n = (ncols + 127) // 128
        if jctr[0] % 4 + n > 4:
            jctr[0] += 4 - jctr[0] % 4
        i = jctr[0] % 4
        jctr[0] += n
        return junk[0:npart, i * 128:i * 128 + ncols], R_junk[i:i + n]

    stH = sb("stH", [128, 4], F32)
    R_stH = [Res(f"stH{i}") for i in range(4)]
    st1 = sb("st1", [128, 48], F32)
    R_st1a = [[Res(f"st1a{p}_{i}") for i in range(2)] for p in range(2)]
    R_st1r = [[Res(f"st1r{p}_{i}") for i in range(2)] for p in range(2)]
    R_st1k = [[Res(f"st1k{p}_{i}") for i in range(2)] for p in range(2)]
    stA = sb("stA", [128, 4], F32)
    R_stA = [Res(f"stA{i}") for i in range(4)]
    stB = sb("stB", [128, 16], F32)
    R_stB = [[Res(f"stB{p}_{i}") for i in range(4)] for p in range(2)]
    nrm = [sb(f"nrm{i}", [128, 384], BF16) for i in range(4)]
    R_nrm = [Res(f"nrm{i}") for i in range(4)]
    nT = [sb(f"nT{i}", [128, 2, 128], BF16) for i in range(4)]
    R_nT = [Res(f"nT{i}") for i in range(4)]
    cqT = sb("cqT", [128, 3, 512], BF16)
    R_cqT = [Res(f"cqT{i}") for i in range(4)]
    knb = [attn[:, :, j * 128:(j + 1) * 128] for j in range(2)]
    R_knb = [Res(f"knb{j}") for j in range(2)]
    krf = [sb(f"krf{i}", [128, 3, 64], F32) for i in range(2)]
    R_krf = [Res(f"krf{i}") for i in range(2)]
    krb2 = [sb(f"krb2{i}", [128, 256], BF16) for i in range(2)]
    R_krb2 = [Res(f"krb2{i}") for i in range(2)]
    qb2 = [[sb(f"qb{p}_{i}", [128, 384], BF16) for i in range(4)] for p in range(2)]
    R_qb2 = [[Res(f"qb{p}_{i}") for i in range(4)] for p in range(2)]
    qrf = [sb(f"qrf{i}", [128, 2, 2, 64], F32) for i in range(4)]
    R_qrf = [Res(f"qrf{i}") for i in range(4)]
    NPB = 4
    Pb = [sb(f"Pb{i}", [128, 512], BF16) for i in range(NPB)]
    R_Pb = [Res(f"Pb{i}") for i in range(NPB)]
    f32w = [sb(f"f32w{i}", [128, 512], F32) for i in range(3)]
    R_f32w = [Res(f"f32w{i}") for i in range(3)]
    fctr = [0]

    def ftile():
        i = fctr[0] % 3
        fctr[0] += 1
        return f32w[i], R_f32w[i]

    mean_t = sb("mean_t", [128, 512], F32)
    R_mean_t = Res("mean_t")
    var_t = sb("var_t", [128, 512], F32)
    R_var_t = Res("var_t")
    b16w = [sb(f"b16w{i}", [128, 512], BF16) for i in range(2)]
    R_b16w = [Res(f"b16w{i}") for i in range(2)]
    bctr = [0]

    def btile():
        i = bctr[0] % 2
        bctr[0] += 1
        return b16w[i], R_b16w[i]

    diag = [sb(f"diag{i}", [128, KW, 128], BF16) for i in range(2)]
    R_diag = [Res(f"diag{i}") for i in range(2)]
    dctr = [0]
    wkv704 = UQ[:, 0:2560].rearrange("p (k n) -> p k n", k=8)
    R_wkv704 = Res("wkv704")
    wukv_sb = UQ[:, 2560:2560 + 4096].rearrange("p (k n) -> p k n", k=2)
    R_wukv = Res("wukv")
    wq_sb = attn[:, :, :].rearrange("p c n -> p (c n)")[:, 0:8 * 384].rearrange("p (k n) -> p k n", k=8)
    R_wq = Res("wq")
    wu_sb = [UQ[:, 6144 + i * 1152:6144 + (i + 1) * 1152].rearrange("p (k n) -> p k n", k=3) for i in range(2)]
    R_wu = [Res(f"wu{i}") for i in range(2)]
    R_alias = [R_wkv704, R_wukv]

    banks = [nc.alloc_psum_tensor(f"pb{i}", [128, 512], F32) for i in range(8)]
    R_bank = [Res(f"bank{i}") for i in range(8)]
    gpool = [list(range(8))]
    gctr = [0]

    def gbank():
        p = gpool[0]
        i = p[gctr[0] % len(p)]
        gctr[0] += 1
        return banks[i], R_bank[i]

    sub_ctr = {"A": 0, "B": 0}

    def sbank(which):
        base = 0 if which == "A" else 4
        i = base + sub_ctr[which] % 4
        sub_ctr[which] += 1
        return banks[i], R_bank[i]

    def bf(bank_ap):
        return bank_ap.bitcast(BF16)

    def interleave(gens):
        gens = list(gens)
        while gens:
            for gen in list(gens):
                try:
                    next(gen)
                except StopIteration:
                    gens.remove(gen)

    def PE_mm(mms, reads, writes):
        mms = list(mms)

        def fn(e):
            ins = None
            for (o, l, r, s0, s1) in mms:
                ins = e.matmul(o, lhsT=l, rhs=r, start=s0, stop=s1)
            return ins

        sch.op("pe", fn, reads, writes)

    def PE_tr(trs, reads, writes):
        trs = list(trs)

        def fn(e):
            ins = None
            for (o, i, k) in trs:
                ins = e.transpose(out=o, in_=i, identity=ident[0:k, 0:k])
            return ins

        sch.op("pe", fn, reads, writes)

    def ACT(out, in_, func, reads, writes, **kw):
        sch.op("act", lambda e: e.activation(out=out, in_=in_, func=func, **kw), reads, writes)

    def ACT_copy(out, in_, reads, writes):
        sch.op("act", lambda e: e.copy(out=out, in_=in_), reads, writes)

    def DVE_tt(out, in0, in1, op, reads, writes):
        sch.op("dve", lambda e: e.tensor_tensor(out=out, in0=in0, in1=in1, op=op), reads, writes)

    def DVE_stt(out, in0, scalar, in1, op0, op1, reads, writes):
        sch.op("dve", lambda e: e.scalar_tensor_tensor(out=out, in0=in0, scalar=scalar, in1=in1,
                                                      op0=op0, op1=op1), reads, writes)

    def DVE_ts(out, in0, s1, s2, op0, op1, reads, writes):
        if op1 is None:
            sch.op("dve", lambda e: e.tensor_scalar(out=out, in0=in0, scalar1=s1, scalar2=None, op0=op0),
                   reads, writes)
        else:
            sch.op("dve", lambda e: e.tensor_scalar(out=out, in0=in0, scalar1=s1, scalar2=s2, op0=op0, op1=op1),
                   reads, writes)

    def rstd_inplace(ap, invn, R_s, npart=128):
        ACT(ap, ap, AF.Ln, [R_const] + R_s, R_s, scale=invn, bias=epst[0:npart, 0:1])
        ACT(ap, ap, AF.Exp, [R_const] + R_s, R_s, scale=-0.5)

    def cload(dst, src):
        sch.dma("sp", dst, src, R_const, writes=[R_const])

    cload(identf[:], d_ident[:, :])
    cload(gbc[:], d_normg[0:1, :].partition_broadcast(128))
    cload(qlg[:], d_qlg[0:1, :].partition_broadcast(128))
    cload(kvlg[:], d_kvlg[0:1, :].partition_broadcast(128))
    cload(gq[:], d_gq[0:1, :].partition_broadcast(128))
    cload(gk[:], d_gk[0:1, :].partition_broadcast(128))
    cload(dwk[:], d_dwk[:, :])
    cload(vecs[:], d_vecs[:, :])
    sch.op("dve", lambda e: e.tensor_copy(out=ident[:], in_=identf[:]), [R_const], [R_const])
    sch.op("dve", lambda e: e.memset(ones_b[:], 1.0), [], [R_const])
    sch.op("dve", lambda e: e.memset(onesN[:], 1.0 / 1024.0), [], [R_const])
    sch.op("dve", lambda e: e.memset(epst[:], EPS), [], [R_const])
    sch.op("dve", lambda e: e.memset(lnc[:], math.log(1.0 / math.sqrt(DH))), [], [R_const])
    DVE_tt(gprod[:], gq[:, 0:128], gk[:, 0:128], ALU.mult, [R_const], [R_const])
    sch.op("dve", lambda e: e.reduce_max(out=negB[:, 1:2], in_=gq[:, :], axis=mybir.AxisListType.X,
                                         apply_absolute_value=True), [R_const], [R_const])
    sch.op("dve", lambda e: e.reduce_max(out=negB[:, 2:3], in_=gk[:, :], axis=mybir.AxisListType.X,
                                         apply_absolute_value=True), [R_const], [R_const])
    DVE_stt(negB[:, 0:1], negB[:, 1:2], -math.sqrt(DH), negB[:, 2:3], ALU.mult, ALU.mult, [R_const], [R_const])
    for j in range(2):
        sch.op("dve", lambda e, j=j: e.memset(krb2[j][:, 64:192], 0.0), [], [R_krb2[j]])
    def gen_diag_setup():
        for c in range(8):
            dg, R_dg = diag[c % 2], R_diag[c % 2]
            for k in range(KW):
                sch.op("dve", lambda e, k=k, c=c, dg=dg: e.tensor_scalar(
                    out=dg[:, k, :], in0=identf[:], scalar1=dwk[:, c * KW + k:c * KW + k + 1],
                    scalar2=None, op0=ALU.mult), [R_const], [R_dg])
                if k % 4 == 3:
                    yield
            sch.dma("sp", d_diag[c], dg[:, :, :].rearrange("p k n -> p (k n)"), R_dscr[c], reads=[R_dg],
                    writes=[R_dscr[c]])
            yield

    def load_diag(c):
        i = dctr[0] % 2
        dctr[0] += 1
        sch.dma("pool", diag[i][:, :, :].rearrange("p k n -> p (k n)"), d_diag[c], R_diag[i],
                reads=[R_dscr[c]], writes=[R_diag[i]])
        return diag[i], R_diag[i]

    xctr = [0]

    def load_h_batch(s, specs):
        load_h_back(load_h_front(s, specs))

    def load_h_front(s, specs):
        bufs = []
        for (row0, dstT, c0, R_dst) in specs:
            i = xctr[0] % 2
            xctr[0] += 1
            sch.dma("sp", xts[i][:, :], x[s, row0:row0 + 128, :], R_xt[i], writes=[R_xt[i]])
            bufs.append(i)
        n = len(specs)
        for j, i in enumerate(bufs):
            ACT(hbs[i][:, :], xts[i][:, :], AF.Square, [R_xt[i]], [R_stH[j], R_hb[i]], accum_out=stH[:, j:j + 1])
        rstd_inplace(stH[:, 0:n], 1.0 / D, R_stH[0:n])
        for j, i in enumerate(bufs):
            DVE_stt(hbs[i][:, :], xts[i][:, :], stH[:, j:j + 1], gbc[:, :], ALU.mult, ALU.mult,
                    [R_xt[i], R_stH[j], R_const], [R_hb[i]])
        return (bufs, specs)

    def load_h_back(state, bank_fn=None):
        bufs, specs = state
        for j, i in enumerate(bufs):
            (row0, dstT, c0, R_dst) = specs[j]
            pb, R_pb = (bank_fn or gbank)()
            pbv = bf(pb[:, :])
            PE_tr([(pbv[:, kc * 128:(kc + 1) * 128], hbs[i][:, kc * 128:(kc + 1) * 128], 128) for kc in range(8)],
                  [R_hb[i], R_const], [R_pb])
            ACT_copy(dstT[:, :, c0:c0 + 128], pbv.rearrange("p (k n) -> p k n", k=8), [], [R_pb, R_dst])

    halo_state = {}

    def load_h_halo_front(s, zero_rows):
        i = xctr[0] % 2
        xctr[0] += 1
        xt, R_x, hb, R_h = xts[i], R_xt[i], hbs[i], R_hb[i]
        halo_state["i"] = i
        sch.op("dve", lambda e: e.memset(xt[0:32, :], 0.0), [], [R_x])
        for (r0, n, p0) in zero_rows:
            sch.dma("sp", xt[p0:p0 + n, :], x[s, r0:r0 + n, :], R_x, writes=[R_x])
        np_ = 30
        ACT(hb[0:np_, :], xt[0:np_, :], AF.Square, [R_x], [R_stH[3], R_h], accum_out=stH[0:np_, 3:4])
        rstd_inplace(stH[0:np_, 3:4], 1.0 / D, [R_stH[3]], npart=np_)
        DVE_stt(hb[0:np_, :], xt[0:np_, :], stH[0:np_, 3:4], gbc[0:np_, :], ALU.mult, ALU.mult,
                [R_x, R_stH[3], R_const], [R_h])

    def load_h_halo_back():
        i = halo_state["i"]
        hb, R_h = hbs[i], R_hb[i]
        np_ = 30
        pb, R_pb = gbank()
        pbv = bf(pb[:, :])
        PE_tr([(pbv[:, kc * 128:kc * 128 + np_], hb[0:np_, kc * 128:(kc + 1) * 128], np_) for kc in range(8)],
              [R_h, R_const], [R_pb])
        ACT_copy(hTh[:, :, 0:np_], pbv.rearrange("p (k n) -> p k n", k=8)[:, :, 0:np_], [], [R_pb, R_hTh])

    allreqs = []
    for _s in range(NSEQ):
        for _g in range(NG):
            for c in range(8):
                allreqs += [(w_in, O_CA + c * 128, 128, 8), (w_in, O_CB + c * 128, 128, 8)]
            allreqs += [(w_in, O_GC + c * 128, 128, 8) for c in range(8)]
            allreqs += [(w_in, O_MC + m * 128, 128, 8) for m in range(8)]
            allreqs += [(w_pw2, m * 128, 128, 8) for m in range(8)]
            allreqs += [(w_in, O_GA + h * 128, 128, 8) for h in range(8)]
            allreqs += [(w_in, O_MA + m * 128, 128, 8) for m in range(8)]
            allreqs += [(w_o, m * 128, 128, 8) for m in range(8)]
            allreqs += [(w_out, cb * 128, 128, 8) for cb in range(8)]
    gws = WStream(allreqs)
    stop = False
    for s in range(NSEQ):
        if stop:
            break
        gpool[0] = list(range(8))
        sch.dma("pool", wkv704, w_in[:, O_CKV:O_CKV + 320].rearrange("(k p) n -> p k n", p=128),
                R_wkv704, writes=[R_wkv704] + R_u + R_cv + R_qT + R_wu + R_sma + R_smc)
        sch.dma("pool", wukv_sb, w_ukv[:, :].rearrange("(k p) n -> p k n", p=128), R_wukv,
                writes=[R_wukv] + R_u + R_cv + R_qT + R_wu + R_sma + R_smc)
        def p1_A(tb):
            par = (tb // 2) % 2
            bA = lambda: sbank("A")
            hs = [2 * par + j for j in range(2)]
            stt = load_h_front(s, [((tb + j) * 128, hT, hs[j] * 128, R_hT[hs[j]]) for j in range(2)])
            yield
            load_h_back(stt, bank_fn=bA)
            yield
            c0 = par * 24
            pbs = []
            for j in range(2):
                pb, R_pb = bA()
                PE_mm([(pb[:, 0:320], hT[:, kc, hs[j] * 128:(hs[j] + 1) * 128], wkv704[:, kc, :], kc == 0, kc == 7)
                       for kc in range(8)], [R_hT[hs[j]], R_wkv704], [R_pb])
                pbs.append((pb, R_pb))
                yield
            for j in range(2):
                pb, R_pb = pbs[j]
                jk, R_jk = jslot(256)
                ACT(jk, pb[:, 0:256], AF.Square, [], [R_pb, R_st1a[par][j]] + R_jk, accum_out=st1[:, c0 + j:c0 + j + 1])
                jk, R_jk = jslot(64)
                ACT(jk, pb[:, 256:320], AF.Square, [], [R_pb, R_st1r[par][j]] + R_jk,
                    accum_out=st1[:, c0 + 2 + j:c0 + 3 + j])
                yield
            rstd_inplace(st1[:, c0:c0 + 2], 1.0 / KVR, R_st1a[par])
            yield
            for j in range(2):
                pb, R_pb = pbs[j]
                n_ = nrm[2 * par + j]
                DVE_stt(n_[:, 0:256], pb[:, 0:256], st1[:, c0 + j:c0 + j + 1], kvlg[:], ALU.mult, ALU.mult,
                        [R_st1a[par][j], R_const], [R_pb, R_nrm[2 * par + j]])
                DVE_tt(krf[j][:, 0, :], pb[:, 256:320], gk[:, 128:192], ALU.mult, [R_const], [R_pb, R_krf[j]])
                yield
            for j in range(2):
                n_ = nrm[2 * par + j]
                pt, R_pt = bA()
                ptv = bf(pt[:, :])
                PE_tr([(ptv[:, kc * 128:(kc + 1) * 128], n_[:, kc * 128:(kc + 1) * 128], 128) for kc in range(2)],
                      [R_nrm[2 * par + j], R_const], [R_pt])
                ACT_copy(nT[2 * par + j][:, :, :], ptv[:, 0:256].rearrange("p (k n) -> p k n", k=2),
                         [], [R_pt, R_nT[2 * par + j]])
                yield
            for j in range(2):
                t = tb + j
                cst, R_cs = load_cs(t)
                DVE_tt(krf[j][:, 1, 0:32], krf[j][:, 0, 32:64], cst[:, 64:96], ALU.mult, [R_cs, R_krf[j]], [R_krf[j]])
                DVE_tt(krf[j][:, 1, 32:64], krf[j][:, 0, 0:32], cst[:, 96:128], ALU.mult, [R_cs, R_krf[j]], [R_krf[j]])
                yield
                DVE_tt(krf[j][:, 2, :], krf[j][:, 0, :], cst[:, 0:64], ALU.mult, [R_cs, R_krf[j]], [R_krf[j]])
                DVE_tt(krb2[j][:, 0:64], krf[j][:, 1, :], krf[j][:, 2, :], ALU.add, [R_krf[j]], [R_krb2[j]])
                DVE_tt(krb2[j][:, 192:256], krf[j][:, 1, :], krf[j][:, 2, :], ALU.add, [R_krf[j]], [R_krb2[j]])
                yield
                pt2, R_pt2 = bA()
                pt2v = bf(pt2[:, :])
                PE_tr([(pt2v[:, 0:128], krb2[j][:, 0:128], 128), (pt2v[:, 128:256], krb2[j][:, 128:256], 128)],
                      [R_krb2[j], R_const], [R_pt2])
                ACT_copy(krTa[:, t * 128:(t + 1) * 128], pt2v[:, 0:128], [], [R_pt2, R_KT[t]])
                ACT_copy(krTb[:, t * 128:(t + 1) * 128], pt2v[:, 128:256], [], [R_pt2, R_KT[t]])
                yield

        def p1_B(tb):
            par = (tb // 2) % 2
            bB = lambda: sbank("B")
            c0 = par * 24
            for j in range(2):
                t = tb + j
                for hp in range(4):
                    pk, R_pk = bB()
                    PE_mm([(pk[:, :], nT[2 * par + j][:, kc, :], wukv_sb[:, kc, hp * 512:(hp + 1) * 512],
                            kc == 0, kc == 1) for kc in range(2)], [R_nT[2 * par + j], R_wukv], [R_pk])
                    pkv = pk[:, :].rearrange("p (i d) -> p i d", i=2)
                    sch.op("dve", lambda e, t=t, hp=hp, pkv=pkv: e.tensor_copy(
                        out=V[:, t, hp * 256:(hp + 1) * 256].rearrange("p (i d) -> p i d", i=2),
                        in_=pkv[:, :, 128:256]), [], [R_pk, R_KT[t]])
                    yield
                    for i2 in range(2):
                        h = 2 * hp + i2
                        jk, R_jk = jslot(128)
                        ACT(jk, pkv[:, i2, 0:128], AF.Square, [], [R_pk, R_st1k[par][j]] + R_jk,
                            accum_out=st1[:, c0 + 8 + j * 8 + h:c0 + 9 + j * 8 + h])
                        DVE_tt(knb[j][:, h, :], pkv[:, i2, 0:128], gprod[:], ALU.mult, [R_const],
                               [R_pk, R_knb[j]] + R_attn)
                    yield
            for j in range(2):
                DVE_ts(st1[:, c0 + 8 + j * 8:c0 + 16 + j * 8], st1[:, c0 + 8 + j * 8:c0 + 16 + j * 8],
                       st1[:, c0 + 2 + j:c0 + 3 + j], None, ALU.add, None,
                       [R_st1r[par][j], R_st1k[par][j]], [R_st1k[par][j]])
            yield
            ACT(st1[:, c0 + 8:c0 + 24], st1[:, c0 + 8:c0 + 24], AF.Ln, [R_const] + R_st1k[par], R_st1k[par],
                scale=1.0 / DH, bias=epst[:, 0:1])
            ACT(sck[:, tb:tb + 2, :], st1[:, c0 + 8:c0 + 24].rearrange("p (j h) -> p j h", j=2), AF.Exp,
                [R_const] + R_st1k[par], [R_KT[tb], R_KT[tb + 1]], scale=-0.5, bias=lnc[:, 0:1])
            yield
            for j in range(2):
                t = tb + j
                pt, R_pt = bB()
                ptv = bf(pt[:, :])
                PE_tr([(ptv[:, h * 128:(h + 1) * 128], knb[j][:, h, :], 128) for h in range(8)],
                      [R_knb[j], R_const], [R_pt])
                sch.op("dve", lambda e, t=t, ptv=ptv: e.tensor_copy(
                    out=KT[:, :, t * 128:(t + 1) * 128], in_=ptv.rearrange("p (k n) -> p k n", k=8)),
                    [], [R_pt, R_KT[t]])
                yield

        tbs = list(range(0, NT, 2))
        extra = [gen_diag_setup()] if s == 0 else []
        interleave([p1_A(tbs[0])] + extra[:0])
        dgen = extra[0] if extra else None
        for ip, tb in enumerate(tbs):
            gens = [p1_B(tb)]
            if ip + 1 < len(tbs):
                gens.insert(0, p1_A(tbs[ip + 1]))
            if dgen is not None:
                gens.append(dgen)
            interleave_partial = gens
            main = [g_ for g_ in gens if g_ is not dgen]
            while main:
                for g_ in list(gens):
                    try:
                        next(g_)
                    except StopIteration:
                        gens.remove(g_)
                        if g_ in main:
                            main.remove(g_)
                        if g_ is dgen:
                            dgen = None
        if dgen is not None:
            interleave([dgen])

        for g in range(NG):
            t0 = g * 512
            gpool[0] = list(range(8))
            reqs = []
            for c in range(8):
                reqs += [(w_in, O_CA + c * 128, 128, 8), (w_in, O_CB + c * 128, 128, 8),
                         (w_in, O_GC + c * 128, 128, 8)]
            for m in range(8):
                reqs += [(w_pw2, m * 128, 128, 8), (w_in, O_MC + m * 128, 128, 8)]
            ws1 = gws
            reqs2 = [(w_in, O_GA + h * 128, 128, 8) for h in range(8)]
            for m in range(8):
                reqs2 += [(w_o, m * 128, 128, 8), (w_in, O_MA + m * 128, 128, 8)]
            reqs2 += [(w_out, cb * 128, 128, 8) for cb in range(8)]
            ws2 = gws

            def emit_halo_front(gg):
                tg0 = gg * 512
                zr = []
                if gg > 0:
                    zr.append((tg0 - 15, 15, 0))
                if gg < NG - 1:
                    zr.append((tg0 + 512, 15, 15))
                load_h_halo_front(s, zr)

            def lh_specs(gg, lo, hi):
                return [(gg * 512 + tt * 128, hT, tt * 128, R_hT[tt]) for tt in range(lo, hi)]

            def emit_load_h_main(gg):
                load_h_batch(s, lh_specs(gg, 0, 2))
                load_h_batch(s, lh_specs(gg, 2, 4))

            if g == 0:
                emit_halo_front(0)
                load_h_halo_back()
                emit_load_h_main(0)

            def conv_AB(c):
                wa, R_wa = ws1.get()
                wb, R_wb = ws1.get()
                pa, R_pa = gbank()
                PE_mm([(pa[:, :], wa[:, kc, :], hT[:, kc, :], kc == 0, kc == 7) for kc in range(8)],
                      R_hT + [R_wa], [R_pa])
                pbk, R_pbk = gbank()
                PE_mm([(pbk[:, :], wb[:, kc, :], hT[:, kc, :], kc == 0, kc == 7) for kc in range(8)],
                      R_hT + [R_wb], [R_pbk])
                ph, R_ph = gbank()
                PE_mm([(ph[:, 0:30], wa[:, kc, :], hTh[:, kc, 0:30], kc == 0, kc == 7) for kc in range(8)]
                      + [(ph[:, 32:62], wb[:, kc, :], hTh[:, kc, 0:30], kc == 0, kc == 7) for kc in range(8)],
                      [R_hTh, R_wa, R_wb], [R_ph])
                sg, R_sg = ftile()
                ACT(sg[:, :], pbk[:, :], AF.Sigmoid, [], [R_pbk, R_sg])
                DVE_tt(u[:, c, 16:528], pa[:, :], sg[:, :], ALU.mult, [R_sg], [R_pa, R_u[c]] + R_qT + R_alias + R_sma + R_smc)
                sgh, R_sgh = ftile()
                ACT(sgh[:, 0:30], ph[:, 32:62], AF.Sigmoid, [], [R_ph, R_sgh])
                DVE_tt(u[:, c, 1:16], ph[:, 0:15], sgh[:, 0:15], ALU.mult, [R_sgh], [R_ph, R_u[c]] + R_smc)
                DVE_tt(u[:, c, 528:543], ph[:, 15:30], sgh[:, 15:30], ALU.mult, [R_sgh], [R_ph, R_u[c]] + R_smc)

            dg_next = load_diag(0)
            conv_AB(0)
            for c in range(8):
                dg, R_dg = dg_next
                if c + 1 < 8:
                    dg_next = load_diag(c + 1)
                    conv_AB(c + 1)
                pc, R_pc = gbank()
                PE_mm([(pc[:, :], dg[:, k, :], u[:, c, k + 1:k + 513], k == 0, k == KW - 1) for k in range(KW)],
                      [R_dg, R_u[c]], [R_pc])
                ACT(cv[:, c, :], pc[:, :], AF.Identity, [R_const], [R_pc, R_cv[c]] + R_qT + R_alias + R_wu + R_sma,
                    bias=vecs[:, c:c + 1])
            if dbg_stop == "cv":
                stop = True
                break
            pm_, R_pm = gbank()
            pq_, R_pq = gbank()
            for c in range(8):
                sq, R_sq = btile()
                ACT(sq[:, :], cv[:, c, :], AF.Square, [R_cv[c]], [R_sq])
                PE_mm([(pm_[:, :], onesN[:], cv[:, c, :], c == 0, c == 7)], [R_cv[c], R_const], [R_pm])
                PE_mm([(pq_[:, :], onesN[:], sq[:, :], c == 0, c == 7)], [R_sq, R_const], [R_pq])
            mean, R_mean = mean_t, R_mean_t
            ACT_copy(mean[:, :], pm_[:, :], [], [R_pm, R_mean])
            var, R_var = var_t, R_var_t
            DVE_tt(var[:, :], mean[:, :], mean[:, :], ALU.mult, [R_mean], [R_var])
            DVE_tt(var[:, :], pq_[:, :], var[:, :], ALU.subtract, [R_var], [R_pq, R_var])
            for c in range(8):
                wg, R_wg = ws1.get()
                pg, R_pg = gbank()
                PE_mm([(pg[:, :], wg[:, kc, :], hT[:, kc, :], kc == 0, kc == 7) for kc in range(8)],
                      R_hT + [R_wg], [R_pg])
                ACT(mc[:, c, :], pg[:, :], AF.Silu, [], [R_pg, R_mc[c]])
            for m in range(8):
                wm, R_wm = ws1.get()
                pg, R_pg = gbank()
                PE_mm([(pg[:, :], wm[:, kc, :], hT[:, kc, :], kc == 0, kc == 7) for kc in range(8)],
                      R_hT + [R_wm], [R_pg])
                ACT(smc[:, m, :], pg[:, :], AF.Sigmoid, [], [R_pg, R_smc[m]] + R_u)
            DVE_ts(var[:, :], var[:, :], 0.0, None, ALU.max, None, [R_var], [R_var])
            ACT(var[:, :], var[:, :], AF.Ln, [R_const, R_var], [R_var], bias=epst[:, 0:1])
            ACT(var[:, :], var[:, :], AF.Exp, [R_var], [R_var], scale=-0.5)
            DVE_tt(mean[:, :], mean[:, :], var[:, :], ALU.mult, [R_var, R_mean], [R_mean])
            wps = [ws1.get(ahead=0) for m in range(8)]
            for c in range(8):
                tq, R_tq = ftile()
                sch.op("pool" if c % 2 else "dve",
                       lambda e, tq=tq, c=c: e.tensor_tensor(out=tq[:, :], in0=cv[:, c, :], in1=var[:, :],
                                                             op=ALU.mult), [R_cv[c], R_var], [R_tq])
                DVE_tt(tq[:, :], tq[:, :], mean[:, :], ALU.subtract, [R_mean, R_tq], [R_tq])
                ACT(tq[:, :], tq[:, :], AF.Silu, [R_const, R_tq], [R_tq], scale=vecs[:, 8 + c:9 + c],
                    bias=vecs[:, 16 + c:17 + c])
                DVE_tt(cv[:, c, :], tq[:, :], mc[:, c, :], ALU.mult, [R_tq, R_mc[c]], [R_cv[c]])
                for m in range(8):
                    wp, R_wp = wps[m]
                    PE_mm([(banks[m][:, :], wp[:, c, :], cv[:, c, :], c == 0, c == 7)], [R_cv[c], R_wp], [R_bank[m]])
            for m in range(8):
                DVE_tt(mc[:, m, :], banks[m][:, :], smc[:, m, :], ALU.mult, [R_smc[m]], [R_bank[m], R_mc[m]])
            gws.prefetch()

            if dbg_stop == "conv":
                stop = True
                break
            wq = wq_sb
            sch.dma("pool", wq, w_in[:, O_CQ:O_CQ + QR].rearrange("(k p) n -> p k n", p=128), R_wq,
                    writes=[R_wq] + R_knb + R_attn)
            pbs = []
            for tt in range(4):
                pb, R_pb = gbank()
                PE_mm([(pb[:, 0:QR], hT[:, kc, tt * 128:(tt + 1) * 128], wq[:, kc, :], kc == 0, kc == 7)
                       for kc in range(8)], [R_hT[tt], R_wq] + R_attn, [R_pb])
                pbs.append((pb, R_pb))
            for tt in range(4):
                pb, R_pb = pbs[tt]
                jk, R_jk = jslot(QR)
                ACT(jk, pb[:, 0:QR], AF.Square, [], [R_pb, R_stA[tt]] + R_jk, accum_out=stA[:, tt:tt + 1])
            rstd_inplace(stA[:, 0:4], 1.0 / QR, R_stA)
            for tt in range(4):
                pb, R_pb = pbs[tt]
                DVE_stt(nrm[tt][:, :], pb[:, 0:QR], stA[:, tt:tt + 1], qlg[:], ALU.mult, ALU.mult,
                        [R_stA[tt], R_const], [R_pb, R_nrm[tt]])
            for tt in range(4):
                pt, R_pt = gbank()
                ptv = bf(pt[:, :])
                PE_tr([(ptv[:, kc * 128:(kc + 1) * 128], nrm[tt][:, kc * 128:(kc + 1) * 128], 128)
                       for kc in range(3)], [R_nrm[tt], R_const], [R_pt])
                ACT_copy(cqT[:, :, tt * 128:(tt + 1) * 128], ptv[:, 0:384].rearrange("p (k n) -> p k n", k=3),
                         [], [R_pt, R_cqT[tt]])
            csl = [load_cs(g * 4 + tt) for tt in range(4)]
            def q_front(hp):
                wu, R_wuc = wu_sb[hp % 2], R_wu[hp % 2]
                sch.dma("pool", wu, w_uq[:, hp * 384:(hp + 1) * 384].rearrange("(k p) n -> p k n", p=128), R_wuc,
                        writes=[R_wuc] + R_cv + R_alias + R_sma)
                qb, R_qb = qb2[hp % 2], R_qb2[hp % 2]
                sB = stB[:, (hp % 2) * 8:(hp % 2) * 8 + 8]
                R_sB = R_stB[hp % 2]
                pbs = []
                for tt in range(4):
                    pb, R_pb = gbank()
                    PE_mm([(pb[:, 0:384], cqT[:, kc, tt * 128:(tt + 1) * 128], wu[:, kc, :], kc == 0, kc == 2)
                           for kc in range(3)], [R_cqT[tt], R_wuc], [R_pb])
                    pbs.append((pb, R_pb))
                for tt in range(4):
                    pb, R_pb = pbs[tt]
                    pbv = pb[:, 0:384].rearrange("p (j d) -> p j d", j=2)
                    for j in range(2):
                        jk, R_jk = jslot(DH)
                        ACT(jk, pbv[:, j, :], AF.Square, [], [R_pb, R_sB[tt]] + R_jk,
                            accum_out=sB[:, tt * 2 + j:tt * 2 + j + 1])
                rstd_inplace(sB[:, 0:8], 1.0 / DH, R_sB)
                for tt in range(4):
                    pb, R_pb = pbs[tt]
                    pbv = pb[:, 0:384].rearrange("p (j d) -> p j d", j=2)
                    rb = sB[:, tt * 2:tt * 2 + 2].unsqueeze(2).broadcast_to([128, 2, 128])
                    DVE_tt(qb[tt][:, 0:256].rearrange("p (j d) -> p j d", j=2), pbv[:, :, 0:128], rb, ALU.mult,
                           [R_sB[tt]], [R_pb, R_qb[tt]])
                    for j in range(2):
                        sc_ = sB[:, tt * 2 + j:tt * 2 + j + 1]
                        DVE_stt(qrf[tt][:, 0, j, :], pbv[:, j, 128:192], sc_, gq[:, 128:192],
                                ALU.mult, ALU.mult, [R_sB[tt], R_const], [R_pb, R_qrf[tt]])
                for tt in range(4):
                    cst, R_cs = csl[tt]
                    q0 = qrf[tt]
                    s1b = cst[:, 64:96].unsqueeze(1).broadcast_to([128, 2, 32])
                    s2b = cst[:, 96:128].unsqueeze(1).broadcast_to([128, 2, 32])
                    csb = cst[:, 0:64].unsqueeze(1).broadcast_to([128, 2, 64])
                    DVE_tt(q0[:, 1, :, 0:32], q0[:, 0, :, 32:64], s1b, ALU.mult, [R_cs, R_qrf[tt]], [R_qrf[tt]])
                    DVE_tt(q0[:, 1, :, 32:64], q0[:, 0, :, 0:32], s2b, ALU.mult, [R_cs, R_qrf[tt]], [R_qrf[tt]])
                    DVE_tt(q0[:, 0, :, :], q0[:, 0, :, :], csb, ALU.mult, [R_cs, R_qrf[tt]], [R_qrf[tt]])
                    DVE_tt(qb[tt][:, 256:384].rearrange("p (j d) -> p j d", j=2), q0[:, 0, :, :], q0[:, 1, :, :],
                           ALU.add, [R_qrf[tt]], [R_qb[tt]])

            def q_back(hp):
                qb, R_qb = qb2[hp % 2], R_qb2[hp % 2]
                for tt in range(4):
                    pt, R_pt = gbank()
                    ptv = bf(pt[:, :])
                    PE_tr([(ptv[:, 0:128], qb[tt][:, 0:128], 128), (ptv[:, 128:256], qb[tt][:, 128:256], 128),
                           (ptv[:, 256:384], qb[tt][:, 256:384], 128)], [R_qb[tt], R_const], [R_pt])
                    ceng = "act" if tt % 2 == 0 else "dve"
                    cp_ = (lambda e, o, i: e.copy(out=o, in_=i)) if ceng == "act" else \
                          (lambda e, o, i: e.tensor_copy(out=o, in_=i))
                    o1 = qT[:, 2 * hp:2 * hp + 2, tt * 128:(tt + 1) * 128]
                    i1 = ptv[:, 0:256].rearrange("p (k n) -> p k n", k=2)
                    o2 = qT[:, 8 + hp, tt * 128:(tt + 1) * 128]
                    i2 = ptv[:, 256:384]
                    wl_ = [R_pt, R_qT[tt]] + R_u + R_cv + R_alias + R_sma + R_smc
                    sch.op(ceng, lambda e, cp_=cp_, o1=o1, i1=i1: cp_(e, o1, i1), [], wl_)
                    sch.op(ceng, lambda e, cp_=cp_, o2=o2, i2=i2: cp_(e, o2, i2), [], wl_)

            q_front(0)
            for hp in range(4):
                if hp + 1 < 4:
                    q_front(hp + 1)
                q_back(hp)

            gpool[0] = [7]
            R_qTall = R_qT
            SB = [0, 1, 6]
            LOOK = 2
            its = [(h, kt) for h in range(8) for kt in range(NT)]

            def emit_S(i):
                h, kt = its[i]
                Sb, R_Sb = banks[SB[i % 3]], R_bank[SB[i % 3]]
                ks = slice(kt * 128, (kt + 1) * 128)
                krX = krTa if h % 2 == 0 else krTb
                PE_mm([(Sb[:, :], KT[:, h, ks], qT[:, h, :], True, False),
                       (Sb[:, :], krX[:, ks], qT[:, 8 + h // 2, :], False, True)],
                      [R_KT[kt]] + R_qTall, [R_Sb])
                P_, R_P = Pb[i % NPB], R_Pb[i % NPB]
                ACT(P_[:, :], Sb[:, :], AF.Exp, [R_KT[kt], R_const], [R_Sb, R_P], scale=sck[:, kt, h:h + 1],
                    bias=negB[:, 0:1])

            for i in range(min(LOOK, len(its))):
                emit_S(i)
            for i, (h, kt) in enumerate(its):
                if i + LOOK < len(its):
                    emit_S(i + LOOK)
                Ob, R_Ob = banks[2 + h % 2], R_bank[2 + h % 2]
                Rb, R_Rb = banks[4 + h % 2], R_bank[4 + h % 2]
                P_, R_P = Pb[i % NPB], R_Pb[i % NPB]
                PE_mm([(Ob[:, :], V[:, kt, h * 128:(h + 1) * 128], P_[:, :], kt == 0, kt == NT - 1)],
                      [R_KT[kt], R_P], [R_Ob])
                PE_mm([(Rb[:, :], ones_b[:], P_[:, :], kt == 0, kt == NT - 1)], [R_P, R_const], [R_Rb])
                if kt == NT - 1:
                    rc, R_rc = ftile()
                    sch.op("dve", lambda e, rc=rc, Rb=Rb: e.reciprocal(out=rc[:, :], in_=Rb[:, :]),
                           [], [R_Rb, R_rc])
                    DVE_tt(attn[:, h, :], Ob[:, :], rc[:, :], ALU.mult, [R_rc],
                           [R_Ob, R_attn[h], R_wq] + R_knb)
            gpool[0] = list(range(8))

            if g + 1 < NG:
                emit_halo_front(g + 1)
            for h in range(8):
                wga, R_wga = ws2.get()
                pg, R_pg = gbank()
                PE_mm([(pg[:, :], wga[:, kc, :], hT[:, kc, :], kc == 0, kc == 7) for kc in range(8)],
                      R_hT + [R_wga], [R_pg])
                sg, R_sg = ftile()
                ACT(sg[:, :], pg[:, :], AF.Silu, [], [R_pg, R_sg])
                DVE_tt(attn[:, h, :], attn[:, h, :], sg[:, :], ALU.mult, [R_sg, R_attn[h]], [R_attn[h]])
            for m in range(8):
                wm, R_wm = ws2.get()
                pg, R_pg = gbank()
                PE_mm([(pg[:, :], wm[:, kc, :], hT[:, kc, :], kc == 0, kc == 7) for kc in range(8)],
                      R_hT + [R_wm], [R_pg])
                ACT(smaF[:, m * 512:(m + 1) * 512], pg[:, :], AF.Sigmoid, [],
                    [R_pg, R_sma[m]] + R_u + R_cv + R_qT + R_wu + R_alias + R_smc)
            lh1 = None
            if g + 1 < NG:
                load_h_halo_back()
                lh1 = load_h_front(s, lh_specs(g + 1, 0, 2))
            for m in range(8):
                wo, R_wo = ws2.get()
                py, R_py = gbank()
                PE_mm([(py[:, :], wo[:, kc, :], attn[:, kc, :], kc == 0, kc == 7) for kc in range(8)],
                      R_attn + [R_wo], [R_py])
                sg, R_sg = ftile()
                DVE_tt(sg[:, :], py[:, :], smaF[:, m * 512:(m + 1) * 512], ALU.mult, [R_sma[m]], [R_py, R_sg])
                DVE_tt(mc[:, m, :], mc[:, m, :], sg[:, :], ALU.add, [R_sg, R_mc[m]], [R_mc[m]])
            lh2 = None
            if lh1 is not None:
                load_h_back(lh1)
                lh2 = load_h_front(s, lh_specs(g + 1, 2, 4))
            wots = [ws2.get(ahead=0) for cb in range(8)]
            for tt in range(4):
                i = xctr[0] % 2
                xctr[0] += 1
                xt, R_x = xts[i], R_xt[i]
                r0 = t0 + tt * 128
                sch.dma("sp", xt[:, :], x[s, r0:r0 + 128, :], R_x, writes=[R_x])
                for cb in range(8):
                    wo_t, R_wot = wots[cb]
                    bi = 2 * tt + cb // 4
                    co = (cb % 4) * 128
                    PE_mm([(banks[bi][:, co:co + 128], mc[:, kc, tt * 128:(tt + 1) * 128], wo_t[:, kc, :],
                            kc == 0, kc == 7) for kc in range(8)], R_mc + [R_wot], [R_bank[bi]])
                for hf in range(2):
                    bi = 2 * tt + hf
                    DVE_tt(xt[:, hf * 512:(hf + 1) * 512], xt[:, hf * 512:(hf + 1) * 512], banks[bi][:, :], ALU.add,
                           [R_x], [R_bank[bi], R_x])
                sch.dma("sp", y[s, r0:r0 + 128, :], xt[:, :], R_x, reads=[R_x])
            gws.prefetch()
            if lh2 is not None:
                load_h_back(lh2)

    sch.final_wait("sp", R_xt)

    with nc.Block() as block:
        @block.sync
        def _(e):
            for f in sch.prog["sp"]:
                f(e)

        @block.vector
        def _(e):
            for f in sch.prog["dve"]:
                f(e)

        @block.tensor
        def _(e):
            for f in sch.prog["pe"]:
                f(e)

        @block.scalar
        def _(e):
            for f in sch.prog["act"]:
                f(e)

        @block.gpsimd
        def _(e):
            for f in sch.prog["pool"]:
                f(e)
    build_nc.sbuf_left = nc.sbuf_bytes_remaining
    return nc


def host_consts(S):
    half = 32
    inv_freq = (1.0 / (np.float32(10000.0) ** (np.arange(half, dtype=np.float32) / np.float32(half)))).astype(np.float32)
    ang = (np.arange(S, dtype=np.float32)[:, None] * inv_freq[None, :]).astype(np.float32)
    cos = np.cos(ang).astype(np.float32)
    sin = np.sin(ang).astype(np.float32)
    return np.ascontiguousarray(np.concatenate([cos, cos, -sin, sin], axis=1).astype(np.float32))


def make_in_maps(xs, S, norm_g, w_in, q_lora_g, w_uq, kv_lora_g, w_ukv, q_head_g, k_head_g, w_o_attn,
                 dw_kernel, dw_bias, conv_ln_g, conv_ln_b, w_pw2, w_out):
    f = lambda a: np.ascontiguousarray(np.asarray(a, dtype=np.float32))
    csss = host_consts(S)
    dwk = f(dw_kernel[0]).T.reshape(8, 128, KW).transpose(1, 0, 2).reshape(128, 8 * KW)
    vec = lambda v: f(v[0]).reshape(8, 128).T
    vecs = np.concatenate([vec(dw_bias), vec(conv_ln_g), vec(conv_ln_b)], axis=1)
    common = {
        "w_in": f(w_in[0]), "w_uq": f(w_uq[0]), "w_ukv": f(w_ukv[0]), "w_o": f(w_o_attn[0]),
        "w_pw2": f(w_pw2[0]), "w_out": f(w_out[0]), "ident": np.eye(128, dtype=np.float32),
        "norm_g": f(norm_g), "q_lora_g": f(q_lora_g), "kv_lora_g": f(kv_lora_g),
        "q_head_g": f(q_head_g), "k_head_g": f(k_head_g), "csss": csss,
        "dwk": f(dwk), "vecs": f(vecs),
    }
    return [dict(common, x=f(xc)) for xc in xs]


def kernel(x_prompt, x_sample, norm_g, w_in, q_lora_g, w_uq, kv_lora_g, w_ukv, q_head_g, k_head_g,
           w_o_attn, dw_kernel, dw_bias, conv_ln_g, conv_ln_b, w_pw2, w_out):
    x_prompt = np.asarray(x_prompt, dtype=np.float32)
    x_sample = np.asarray(x_sample, dtype=np.float32)
    S = x_prompt.shape[1]
    xs = [np.concatenate([x_prompt[c:c + 1], x_sample[2 * c:2 * c + 2]], axis=0) for c in range(NCORES)]
    nc = build_nc(S, 3)
    in_maps = make_in_maps(xs, S, norm_g, w_in, q_lora_g, w_uq, kv_lora_g, w_ukv, q_head_g, k_head_g,
                           w_o_attn, dw_kernel, dw_bias, conv_ln_g, conv_ln_b, w_pw2, w_out)
    res = run_bass_kernel_spmd(nc, in_maps, core_ids=list(range(NCORES)))
    ys = [np.asarray(r["y"], dtype=np.float32) for r in res.results]
    y_prompt = np.stack([ys[c][0] for c in range(NCORES)], axis=0)
    y_sample = np.stack([ys[c][1 + j] for c in range(NCORES) for j in range(2)], axis=0)
    return (y_prompt, y_sample)
```

```python
import math
import numpy as np
import concourse.bass as bass
import concourse.mybir as mybir
from concourse.bass_utils import run_bass_kernel_spmd

F32 = mybir.dt.float32
BF16 = mybir.dt.bfloat16
ALU = mybir.AluOpType
AF = mybir.ActivationFunctionType

D = 1024
NH = 8
DH = 192
QR = 384
KVR = 256
DIN = 6848
O_CQ, O_CKV, O_KR, O_GA, O_CA, O_CB, O_GC, O_MA, O_MC = 0, 384, 640, 704, 1728, 2752, 3776, 4800, 5824
EPS = 1e-6
NCORES = 8
KW = 31


class Res:
    __slots__ = ("name", "w", "r", "dsem", "dcnt")

    def __init__(self, name):
        self.name = name
        self.w = None
        self.r = {}
        self.dsem = None
        self.dcnt = 0


class Sched:
    ENG = ("pe", "act", "dve", "pool", "sp")

    def __init__(self, nc):
        self.nc = nc
        self.sems = {e: nc.alloc_semaphore("tl_" + e) for e in self.ENG}
        self.cnt = {e: 0 for e in self.ENG}
        self.seen = {e: {} for e in self.ENG}
        self.prog = {e: [] for e in self.ENG}

    def _collect(self, eng, reads, writes):
        need = {}

        def add(ev, raw):
            if ev is None:
                return
            k, v, clk = ev
            if k not in need or need[k][0] < v:
                need[k] = (v, clk)

        for r in reads:
            add(r.w, True)
        for w in writes:
            add(w.w, False)
            for ev in w.r.values():
                add(ev, False)
        seen = self.seen[eng]
        wl = []
        for k, (v, clk) in need.items():
            if k == eng and eng == "pe":
                continue
            if seen.get(k, 0) >= v:
                continue
            wl.append((self.sems[k], v))
            seen[k] = v
            for k2, v2 in clk.items():
                if seen.get(k2, 0) < v2:
                    seen[k2] = v2
        return wl

    def op(self, eng, fn, reads=(), writes=()):
        wl = self._collect(eng, reads, writes)
        self.cnt[eng] += 1
        n = self.cnt[eng]
        semh = self.sems[eng]

        def emit(e):
            for s, v in wl:
                e.wait_ge(s, v)
            fn(e).then_inc(semh, 1)

        self.prog[eng].append(emit)
        clk = dict(self.seen[eng])
        clk[eng] = n
        ev = (eng, n, clk)
        for r in reads:
            r.r[eng] = ev
        for w in writes:
            w.w = ev
            w.r = {}

    def dma(self, eng, out, in_, semres, reads=(), writes=(), **kw):
        wl = self._collect(eng, reads, writes)
        R = semres
        if R.dsem is None:
            R.dsem = "d:" + R.name
            self.sems[R.dsem] = self.nc.alloc_semaphore("d_" + R.name)
        R.dcnt += 16
        v = R.dcnt
        semh = self.sems[R.dsem]

        def emit(e):
            for s, vv in wl:
                e.wait_ge(s, vv)
            e.dma_start(out=out, in_=in_, **kw).then_inc(semh, 16)

        self.prog[eng].append(emit)
        ev = (R.dsem, v, dict(self.seen[eng]))
        for r in reads:
            r.r[R.dsem] = ev
        for w in writes:
            w.w = ev
            w.r = {}

    def final_wait(self, eng, ress):
        wl = self._collect(eng, (), ress)

        def emit(e):
            for s, v in wl:
                e.wait_ge(s, v)

        self.prog[eng].append(emit)


def build_nc(S, NSEQ, dbg_stop=None):
    NT = S // 128
    NG = S // 512
    nc = bass.Bass("TRN2", target_bir_lowering=False)
    sch = Sched(nc)

    def din(name, shape):
        return nc.dram_tensor(name, list(shape), F32, kind="ExternalInput").ap()

    x = din("x", [NSEQ, S, D])
    y = nc.dram_tensor("y", [NSEQ, S, D], F32, kind="ExternalOutput").ap()
    w_in = din("w_in", [D, DIN])
    w_uq = din("w_uq", [QR, NH * DH])
    w_ukv = din("w_ukv", [KVR, NH * 256])
    w_o = din("w_o", [D, D])
    w_pw2 = din("w_pw2", [D, D])
    w_out = din("w_out", [D, D])
    d_ident = din("ident", [128, 128])
    d_normg = din("norm_g", [1, D])
    d_qlg = din("q_lora_g", [1, QR])
    d_kvlg = din("kv_lora_g", [1, KVR])
    d_gq = din("q_head_g", [1, DH])
    d_gk = din("k_head_g", [1, DH])
    d_cs = din("csss", [S, 128])
    d_dwk = din("dwk", [128, 8 * KW])
    d_vecs = din("vecs", [128, 24])
    d_diag = nc.dram_tensor("diag_scr", [8, 128, KW * 128], BF16, kind="Internal").ap()
    R_dscr = [Res(f"dscr{c}") for c in range(8)]

    def sb(name, shape, dt):
        return nc.alloc_sbuf_tensor("s_" + name, list(shape), dt)

    ident = sb("ident", [128, 128], BF16)
    identf = sb("identf", [128, 128], F32)
    ones_b = sb("ones_b", [128, 128], BF16)
    onesN = sb("onesN", [128, 128], BF16)
    gbc = sb("gbc", [128, D], F32)
    qlg = sb("qlg", [128, QR], F32)
    kvlg = sb("kvlg", [128, KVR], F32)
    gq = sb("gq", [128, DH], F32)
    gk = sb("gk", [128, DH], F32)
    gprod = sb("gprod", [128, 128], F32)
    NCS = 4
    csts = [sb(f"cst{i}", [128, 128], F32) for i in range(NCS)]
    R_cst = [Res(f"cst{i}") for i in range(NCS)]
    cctr = [0]

    def load_cs(t):
        i = cctr[0] % NCS
        cctr[0] += 1
        sch.dma("sp", csts[i][:, :], d_cs[t * 128:(t + 1) * 128, :], R_cst[i], writes=[R_cst[i]])
        return csts[i], R_cst[i]

    dwk = sb("dwk", [128, 8 * KW], F32)
    vecs = sb("vecs", [128, 24], F32)
    epst = sb("epst", [128, 1], F32)
    lnc = sb("lnc", [128, 1], F32)
    negB = sb("negB", [128, 4], F32)
    R_const = Res("const")

    KT = sb("KT", [128, NH, S], BF16)
    krTa = sb("krTa", [128, S], BF16)
    krTb = sb("krTb", [128, S], BF16)
    V = sb("V", [128, NT, D], BF16)
    sck = sb("sck", [128, NT, NH], F32)
    R_KT = [Res(f"KT{t}") for t in range(NT)]

    hT = sb("hT", [128, 8, 512], BF16)
    hTh = sb("hTh", [128, 8, 32], BF16)
    R_hT = [Res(f"hT{i}") for i in range(4)]
    R_hTh = Res("hTh")
    xts = [sb(f"xt{i}", [128, D], F32) for i in range(2)]
    R_xt = [Res(f"xt{i}") for i in range(2)]
    hbs = [sb(f"hb{i}", [128, D], BF16) for i in range(2)]
    R_hb = [Res(f"hb{i}") for i in range(2)]
    UQ = sb("UQ", [128, 8 * 544 + 8 * 512], BF16)
    u = UQ[:, 0:8 * 544].rearrange("p (c n) -> p c n", c=8)
    cv = UQ[:, 8 * 544:8 * 544 + 8 * 512].rearrange("p (c n) -> p c n", c=8)
    qT = UQ[:, 0:12 * 512].rearrange("p (c n) -> p c n", c=12)
    R_u = [Res(f"u{c}") for c in range(8)]
    R_cv = [Res(f"cv{c}") for c in range(8)]
    R_qT = [Res(f"qT{i}") for i in range(4)]
    smaF = UQ[:, :].bitcast(F32)
    smc = UQ[:, 0:8 * 512].rearrange("p (c n) -> p c n", c=8)
    R_smc = [Res(f"smc{i}") for i in range(8)]
    R_sma = [Res(f"sma{i}") for i in range(8)]
    attn = sb("attn", [128, 8, 512], BF16)
    R_attn = [Res(f"attn{h}") for h in range(8)]
    mc = sb("mc", [128, 8, 512], BF16)
    R_mc = [Res(f"mc{c}") for c in range(8)]

    NSLOT = 8
    AHEAD = 5
    wslots = [sb(f"ws{i}", [128, 8 * 128], BF16) for i in range(NSLOT)]
    R_ws = [Res(f"ws{i}") for i in range(NSLOT)]
    wctr = [0]

    def wtile(src, col0, ncols, KC):
        i = wctr[0] % NSLOT
        wctr[0] += 1
        view = wslots[i][:, 0:KC * ncols].rearrange("p (k n) -> p k n", k=KC)
        sch.dma("pool", view, src[:, col0:col0 + ncols].rearrange("(k p) n -> p k n", p=128),
                R_ws[i], writes=[R_ws[i]])
        return view, R_ws[i]

    class WStream:
        def __init__(self, reqs):
            self.reqs = reqs
            self.issued = []
            self.pos = 0

        def prefetch(self):
            while len(self.issued) < min(len(self.reqs), self.pos + 1 + AHEAD):
                self.issued.append(wtile(*self.reqs[len(self.issued)]))

        def get(self, ahead=AHEAD):
            while len(self.issued) < min(len(self.reqs), self.pos + 1 + ahead):
                self.issued.append(wtile(*self.reqs[len(self.issued)]))
            r = self.issued[self.pos]
            self.pos += 1
            return r

    junk = sb("junk", [128, 512], BF16)
    R_junk = [Res(f"junk{i}") for i in range(4)]
    jctr = [0]

    def jslot(ncols, npart=128):
        n = (ncols + 127) // 128
        if jctr[0] % 4 + n > 4:
            jctr[0] += 4 - jctr[0] % 4
        i = jctr[0] % 4
        jctr[0] += n
        return junk[0:npart, i * 128:i * 128 + ncols], R_junk[i:i + n]

    stH = sb("stH", [128, 4], F32)
    R_stH = [Res(f"stH{i}") for i in range(4)]
    st1 = sb("st1", [128, 48], F32)
    R_st1a = [[Res(f"st1a{p}_{i}") for i in range(2)] for p in range(2)]
    R_st1r = [[Res(f"st1r{p}_{i}") for i in range(2)] for p in range(2)]
    R_st1k = [[Res(f"st1k{p}_{i}") for i in range(2)] for p in range(2)]
    stA = sb("stA", [128, 4], F32)
    R_stA = [Res(f"stA{i}") for i in range(4)]
    stB = sb("stB", [128, 16], F32)
    R_stB = [[Res(f"stB{p}_{i}") for i in range(4)] for p in range(2)]
    nrm = [sb(f"nrm{i}", [128, 384], BF16) for i in range(4)]
    R_nrm = [Res(f"nrm{i}") for i in range(4)]
    nT = [sb(f"nT{i}", [128, 2, 128], BF16) for i in range(4)]
    R_nT = [Res(f"nT{i}") for i in range(4)]
    cqT = sb("cqT", [128, 3, 512], BF16)
    R_cqT = [Res(f"cqT{i}") for i in range(4)]
    knb = [attn[:, :, j * 128:(j + 1) * 128] for j in range(2)]
    R_knb = [Res(f"knb{j}") for j in range(2)]
    krf = [sb(f"krf{i}", [128, 3, 64], F32) for i in range(2)]
    R_krf = [Res(f"krf{i}") for i in range(2)]
    krb2 = [sb(f"krb2{i}", [128, 256], BF16) for i in range(2)]
    R_krb2 = [Res(f"krb2{i}") for i in range(2)]
    qb2 = [[sb(f"qb{p}_{i}", [128, 384], BF16) for i in range(4)] for p in range(2)]
    R_qb2 = [[Res(f"qb{p}_{i}") for i in range(4)] for p in range(2)]
    qrf = [sb(f"qrf{i}", [128, 2, 2, 64], F32) for i in range(4)]
    R_qrf = [Res(f"qrf{i}") for i in range(4)]
    NPB = 4
    Pb = [sb(f"Pb{i}", [128, 512], BF16) for i in range(NPB)]
    R_Pb = [Res(f"Pb{i}") for i in range(NPB)]
    f32w = [sb(f"f32w{i}", [128, 512], F32) for i in range(3)]
    R_f32w = [Res(f"f32w{i}") for i in range(3)]
    fctr = [0]

    def ftile():
        i = fctr[0] % 3
        fctr[0] += 1
        return f32w[i], R_f32w[i]

    mean_t = sb("mean_t", [128, 512], F32)
    R_mean_t = Res("mean_t")
    var_t = sb("var_t", [128, 512], F32)
    R_var_t = Res("var_t")
    b16w = [sb(f"b16w{i}", [128, 512], BF16) for i in range(2)]
    R_b16w = [Res(f"b16w{i}") for i in range(2)]
    bctr = [0]

    def btile():
        i = bctr[0] % 2
        bctr[0] += 1
        return b16w[i], R_b16w[i]

    diag = [sb(f"diag{i}", [128, KW, 128], BF16) for i in range(2)]
    R_diag = [Res(f"diag{i}") for i in range(2)]
    dctr = [0]
    wkv704 = UQ[:, 0:2560].rearrange("p (k n) -> p k n", k=8)
    R_wkv704 = Res("wkv704")
    wukv_sb = UQ[:, 2560:2560 + 4096].rearrange("p (k n) -> p k n", k=2)
    R_wukv = Res("wukv")
    wq_sb = attn[:, :, :].rearrange("p c n -> p (c n)")[:, 0:8 * 384].rearrange("p (k n) -> p k n", k=8)
    R_wq = Res("wq")
    wu_sb = [UQ[:, 6144 + i * 1152:6144 + (i + 1) * 1152].rearrange("p (k n) -> p k n", k=3) for i in range(2)]
    R_wu = [Res(f"wu{i}") for i in range(2)]
    R_alias = [R_wkv704, R_wukv]

    banks = [nc.alloc_psum_tensor(f"pb{i}", [128, 512], F32) for i in range(8)]
    R_bank = [Res(f"bank{i}") for i in range(8)]
    gpool = [list(range(8))]
    gctr = [0]

    def gbank():
        p = gpool[0]
        i = p[gctr[0] % len(p)]
        gctr[0] += 1
        return banks[i], R_bank[i]

    sub_ctr = {"A": 0, "B": 0}

    def sbank(which):
        base = 0 if which == "A" else 4
        i = base + sub_ctr[which] % 4
        sub_ctr[which] += 1
        return banks[i], R_bank[i]

    def bf(bank_ap):
        return bank_ap.bitcast(BF16)

    def interleave(gens):
        gens = list(gens)
        while gens:
            for gen in list(gens):
                try:
                    next(gen)
                except StopIteration:
                    gens.remove(gen)

    def PE_mm(mms, reads, writes):
        mms = list(mms)

        def fn(e):
            ins = None
            for (o, l, r, s0, s1) in mms:
                ins = e.matmul(o, lhsT=l, rhs=r, start=s0, stop=s1)
            return ins

        sch.op("pe", fn, reads, writes)

    def PE_tr(trs, reads, writes):
        trs = list(trs)

        def fn(e):
            ins = None
            for (o, i, k) in trs:
                ins = e.transpose(out=o, in_=i, identity=ident[0:k, 0:k])
            return ins

        sch.op("pe", fn, reads, writes)

    def ACT(out, in_, func, reads, writes, **kw):
        sch.op("act", lambda e: e.activation(out=out, in_=in_, func=func, **kw), reads, writes)

    def ACT_copy(out, in_, reads, writes):
        sch.op("act", lambda e: e.copy(out=out, in_=in_), reads, writes)

    def DVE_tt(out, in0, in1, op, reads, writes):
        sch.op("dve", lambda e: e.tensor_tensor(out=out, in0=in0, in1=in1, op=op), reads, writes)

    def DVE_stt(out, in0, scalar, in1, op0, op1, reads, writes):
        sch.op("dve", lambda e: e.scalar_tensor_tensor(out=out, in0=in0, scalar=scalar, in1=in1,
                                                      op0=op0, op1=op1), reads, writes)

    def DVE_ts(out, in0, s1, s2, op0, op1, reads, writes):
        if op1 is None:
            sch.op("dve", lambda e: e.tensor_scalar(out=out, in0=in0, scalar1=s1, scalar2=None, op0=op0),
                   reads, writes)
        else:
            sch.op("dve", lambda e: e.tensor_scalar(out=out, in0=in0, scalar1=s1, scalar2=s2, op0=op0, op1=op1),
                   reads, writes)

    def rstd_inplace(ap, invn, R_s, npart=128):
        ACT(ap, ap, AF.Ln, [R_const] + R_s, R_s, scale=invn, bias=epst[0:npart, 0:1])
        ACT(ap, ap, AF.Exp, [R_const] + R_s, R_s, scale=-0.5)

    def cload(dst, src):
        sch.dma("sp", dst, src, R_const, writes=[R_const])

    cload(identf[:], d_ident[:, :])
    cload(gbc[:], d_normg[0:1, :].partition_broadcast(128))
    cload(qlg[:], d_qlg[0:1, :].partition_broadcast(128))
    cload(kvlg[:], d_kvlg[0:1, :].partition_broadcast(128))
    cload(gq[:], d_gq[0:1, :].partition_broadcast(128))
    cload(gk[:], d_gk[0:1, :].partition_broadcast(128))
    cload(dwk[:], d_dwk[:, :])
    cload(vecs[:], d_vecs[:, :])
    sch.op("dve", lambda e: e.tensor_copy(out=ident[:], in_=identf[:]), [R_const], [R_const])
    sch.op("dve", lambda e: e.memset(ones_b[:], 1.0), [], [R_const])
    sch.op("dve", lambda e: e.memset(onesN[:], 1.0 / 1024.0), [], [R_const])
    sch.op("dve", lambda e: e.memset(epst[:], EPS), [], [R_const])
    sch.op("dve", lambda e: e.memset(lnc[:], math.log(1.0 / math.sqrt(DH))), [], [R_const])
    DVE_tt(gprod[:], gq[:, 0:128], gk[:, 0:128], ALU.mult, [R_const], [R_const])
    sch.op("dve", lambda e: e.reduce_max(out=negB[:, 1:2], in_=gq[:, :], axis=mybir.AxisListType.X,
                                         apply_absolute_value=True), [R_const], [R_const])
    sch.op("dve", lambda e: e.reduce_max(out=negB[:, 2:3], in_=gk[:, :], axis=mybir.AxisListType.X,
                                         apply_absolute_value=True), [R_const], [R_const])
    DVE_stt(negB[:, 0:1], negB[:, 1:2], -math.sqrt(DH), negB[:, 2:3], ALU.mult, ALU.mult, [R_const], [R_const])
    for j in range(2):
        sch.op("dve", lambda e, j=j: e.memset(krb2[j][:, 64:192], 0.0), [], [R_krb2[j]])
    def gen_diag_setup():
        for c in range(8):
            dg, R_dg = diag[c % 2], R_diag[c % 2]
            for k in range(KW):
                sch.op("dve", lambda e, k=k, c=c, dg=dg: e.tensor_scalar(
                    out=dg[:, k, :], in0=identf[:], scalar1=dwk[:, c * KW + k:c * KW + k + 1],
                    scalar2=None, op0=ALU.mult), [R_const], [R_dg])
                if k % 4 == 3:
                    yield
            sch.dma("sp", d_diag[c], dg[:, :, :].rearrange("p k n -> p (k n)"), R_dscr[c], reads=[R_dg],
                    writes=[R_dscr[c]])
            yield

    def load_diag(c):
        i = dctr[0] % 2
        dctr[0] += 1
        sch.dma("pool", diag[i][:, :, :].rearrange("p k n -> p (k n)"), d_diag[c], R_diag[i],
                reads=[R_dscr[c]], writes=[R_diag[i]])
        return diag[i], R_diag[i]

    xctr = [0]

    def load_h_batch(s, specs):
        load_h_back(load_h_front(s, specs))

    def load_h_front(s, specs):
        bufs = []
        for (row0, dstT, c0, R_dst) in specs:
            i = xctr[0] % 2
            xctr[0] += 1
            sch.dma("sp", xts[i][:, :], x[s, row0:row0 + 128, :], R_xt[i], writes=[R_xt[i]])
            bufs.append(i)
        n = len(specs)
        for j, i in enumerate(bufs):
            ACT(hbs[i][:, :], xts[i][:, :], AF.Square, [R_xt[i]], [R_stH[j], R_hb[i]], accum_out=stH[:, j:j + 1])
        rstd_inplace(stH[:, 0:n], 1.0 / D, R_stH[0:n])
        for j, i in enumerate(bufs):
            DVE_stt(hbs[i][:, :], xts[i][:, :], stH[:, j:j + 1], gbc[:, :], ALU.mult, ALU.mult,
                    [R_xt[i], R_stH[j], R_const], [R_hb[i]])
        return (bufs, specs)

    def load_h_back(state, bank_fn=None):
        bufs, specs = state
        for j, i in enumerate(bufs):
            (row0, dstT, c0, R_dst) = specs[j]
            pb, R_pb = (bank_fn or gbank)()
            pbv = bf(pb[:, :])
            PE_tr([(pbv[:, kc * 128:(kc + 1) * 128], hbs[i][:, kc * 128:(kc + 1) * 128], 128) for kc in range(8)],
                  [R_hb[i], R_const], [R_pb])
            ACT_copy(dstT[:, :, c0:c0 + 128], pbv.rearrange("p (k n) -> p k n", k=8), [], [R_pb, R_dst])

    halo_state = {}

    def load_h_halo_front(s, zero_rows):
        i = xctr[0] % 2
        xctr[0] += 1
        xt, R_x, hb, R_h = xts[i], R_xt[i], hbs[i], R_hb[i]
        halo_state["i"] = i
        sch.op("dve", lambda e: e.memset(xt[0:32, :], 0.0), [], [R_x])
        for (r0, n, p0) in zero_rows:
            sch.dma("sp", xt[p0:p0 + n, :], x[s, r0:r0 + n, :], R_x, writes=[R_x])
        np_ = 30
        ACT(hb[0:np_, :], xt[0:np_, :], AF.Square, [R_x], [R_stH[3], R_h], accum_out=stH[0:np_, 3:4])
        rstd_inplace(stH[0:np_, 3:4], 1.0 / D, [R_stH[3]], npart=np_)
        DVE_stt(hb[0:np_, :], xt[0:np_, :], stH[0:np_, 3:4], gbc[0:np_, :], ALU.mult, ALU.mult,
                [R_x, R_stH[3], R_const], [R_h])

    def load_h_halo_back():
        i = halo_state["i"]
        hb, R_h = hbs[i], R_hb[i]
        np_ = 30
        pb, R_pb = gbank()
        pbv = bf(pb[:, :])
        PE_tr([(pbv[:, kc * 128:kc * 128 + np_], hb[0:np_, kc * 128:(kc + 1) * 128], np_) for kc in range(8)],
              [R_h, R_const], [R_pb])
        ACT_copy(hTh[:, :, 0:np_], pbv.rearrange("p (k n) -> p k n", k=8)[:, :, 0:np_], [], [R_pb, R_hTh])

    allreqs = []
    for _s in range(NSEQ):
        for _g in range(NG):
            for c in range(8):
                allreqs += [(w_in, O_CA + c * 128, 128, 8), (w_in, O_CB + c * 128, 128, 8)]
            allreqs += [(w_in, O_GC + c * 128, 128, 8) for c in range(8)]
            allreqs += [(w_in, O_MC + m * 128, 128, 8) for m in range(8)]
            allreqs += [(w_pw2, m * 128, 128, 8) for m in range(8)]
            allreqs += [(w_in, O_GA + h * 128, 128, 8) for h in range(8)]
            allreqs += [(w_in, O_MA + m * 128, 128, 8) for m in range(8)]
            allreqs += [(w_o, m * 128, 128, 8) for m in range(8)]
            allreqs += [(w_out, cb * 128, 128, 8) for cb in range(8)]
    gws = WStream(allreqs)
    stop = False
    for s in range(NSEQ):
        if stop:
            break
        gpool[0] = list(range(8))
        sch.dma("pool", wkv704, w_in[:, O_CKV:O_CKV + 320].rearrange("(k p) n -> p k n", p=128),
                R_wkv704, writes=[R_wkv704] + R_u + R_cv + R_qT + R_wu + R_sma + R_smc)
        sch.dma("pool", wukv_sb, w_ukv[:, :].rearrange("(k p) n -> p k n", p=128), R_wukv,
                writes=[R_wukv] + R_u + R_cv + R_qT + R_wu + R_sma + R_smc)
        def p1_A(tb):
            par = (tb // 2) % 2
            bA = lambda: sbank("A")
            hs = [2 * par + j for j in range(2)]
            stt = load_h_front(s, [((tb + j) * 128, hT, hs[j] * 128, R_hT[hs[j]]) for j in range(2)])
            yield
            load_h_back(stt, bank_fn=bA)
            yield
            c0 = par * 24
            pbs = []
            for j in range(2):
                pb, R_pb = bA()
                PE_mm([(pb[:, 0:320], hT[:, kc, hs[j] * 128:(hs[j] + 1) * 128], wkv704[:, kc, :], kc == 0, kc == 7)
                       for kc in range(8)], [R_hT[hs[j]], R_wkv704], [R_pb])
                pbs.append((pb, R_pb))
                yield
            for j in range(2):
                pb, R_pb = pbs[j]
                jk, R_jk = jslot(256)
                ACT(jk, pb[:, 0:256], AF.Square, [], [R_pb, R_st1a[par][j]] + R_jk, accum_out=st1[:, c0 + j:c0 + j + 1])
                jk, R_jk = jslot(64)
                ACT(jk, pb[:, 256:320], AF.Square, [], [R_pb, R_st1r[par][j]] + R_jk,
                    accum_out=st1[:, c0 + 2 + j:c0 + 3 + j])
                yield
            rstd_inplace(st1[:, c0:c0 + 2], 1.0 / KVR, R_st1a[par])
            yield
            for j in range(2):
                pb, R_pb = pbs[j]
                n_ = nrm[2 * par + j]
                DVE_stt(n_[:, 0:256], pb[:, 0:256], st1[:, c0 + j:c0 + j + 1], kvlg[:], ALU.mult, ALU.mult,
                        [R_st1a[par][j], R_const], [R_pb, R_nrm[2 * par + j]])
                DVE_tt(krf[j][:, 0, :], pb[:, 256:320], gk[:, 128:192], ALU.mult, [R_const], [R_pb, R_krf[j]])
                yield
            for j in range(2):
                n_ = nrm[2 * par + j]
                pt, R_pt = bA()
                ptv = bf(pt[:, :])
                PE_tr([(ptv[:, kc * 128:(kc + 1) * 128], n_[:, kc * 128:(kc + 1) * 128], 128) for kc in range(2)],
                      [R_nrm[2 * par + j], R_const], [R_pt])
                ACT_copy(nT[2 * par + j][:, :, :], ptv[:, 0:256].rearrange("p (k n) -> p k n", k=2),
                         [], [R_pt, R_nT[2 * par + j]])
                yield
            for j in range(2):
                t = tb + j
                cst, R_cs = load_cs(t)
                DVE_tt(krf[j][:, 1, 0:32], krf[j][:, 0, 32:64], cst[:, 64:96], ALU.mult, [R_cs, R_krf[j]], [R_krf[j]])
                DVE_tt(krf[j][:, 1, 32:64], krf[j][:, 0, 0:32], cst[:, 96:128], ALU.mult, [R_cs, R_krf[j]], [R_krf[j]])
                yield
                DVE_tt(krf[j][:, 2, :], krf[j][:, 0, :], cst[:, 0:64], ALU.mult, [R_cs, R_krf[j]], [R_krf[j]])
                DVE_tt(krb2[j][:, 0:64], krf[j][:, 1, :], krf[j][:, 2, :], ALU.add, [R_krf[j]], [R_krb2[j]])
                DVE_tt(krb2[j][:, 192:256], krf[j][:, 1, :], krf[j][:, 2, :], ALU.add, [R_krf[j]], [R_krb2[j]])
                yield
                pt2, R_pt2 = bA()
                pt2v = bf(pt2[:, :])
                PE_tr([(pt2v[:, 0:128], krb2[j][:, 0:128], 128), (pt2v[:, 128:256], krb2[j][:, 128:256], 128)],
                      [R_krb2[j], R_const], [R_pt2])
                ACT_copy(krTa[:, t * 128:(t + 1) * 128], pt2v[:, 0:128], [], [R_pt2, R_KT[t]])
                ACT_copy(krTb[:, t * 128:(t + 1) * 128], pt2v[:, 128:256], [], [R_pt2, R_KT[t]])
                yield

        def p1_B(tb):
            par = (tb // 2) % 2
            bB = lambda: sbank("B")
            c0 = par * 24
            for j in range(2):
                t = tb + j
                for hp in range(4):
                    pk, R_pk = bB()
                    PE_mm([(pk[:, :], nT[2 * par + j][:, kc, :], wukv_sb[:, kc, hp * 512:(hp + 1) * 512],
                            kc == 0, kc == 1) for kc in range(2)], [R_nT[2 * par + j], R_wukv], [R_pk])
                    pkv = pk[:, :].rearrange("p (i d) -> p i d", i=2)
                    sch.op("dve", lambda e, t=t, hp=hp, pkv=pkv: e.tensor_copy(
                        out=V[:, t, hp * 256:(hp + 1) * 256].rearrange("p (i d) -> p i d", i=2),
                        in_=pkv[:, :, 128:256]), [], [R_pk, R_KT[t]])
                    yield
                    for i2 in range(2):
                        h = 2 * hp + i2
                        jk, R_jk = jslot(128)
                        ACT(jk, pkv[:, i2, 0:128], AF.Square, [], [R_pk, R_st1k[par][j]] + R_jk,
                            accum_out=st1[:, c0 + 8 + j * 8 + h:c0 + 9 + j * 8 + h])
                    DVE_tt(knb[j][:, 2 * hp:2 * hp + 2, :], pkv[:, :, 0:128],
                           gprod[:, :].unsqueeze(1).broadcast_to([128, 2, 128]), ALU.mult, [R_const],
                           [R_pk, R_knb[j]] + R_attn)
                    yield
            for j in range(2):
                DVE_ts(st1[:, c0 + 8 + j * 8:c0 + 16 + j * 8], st1[:, c0 + 8 + j * 8:c0 + 16 + j * 8],
                       st1[:, c0 + 2 + j:c0 + 3 + j], None, ALU.add, None,
                       [R_st1r[par][j], R_st1k[par][j]], [R_st1k[par][j]])
            yield
            ACT(st1[:, c0 + 8:c0 + 24], st1[:, c0 + 8:c0 + 24], AF.Ln, [R_const] + R_st1k[par], R_st1k[par],
                scale=1.0 / DH, bias=epst[:, 0:1])
            ACT(sck[:, tb:tb + 2, :], st1[:, c0 + 8:c0 + 24].rearrange("p (j h) -> p j h", j=2), AF.Exp,
                [R_const] + R_st1k[par], [R_KT[tb], R_KT[tb + 1]], scale=-0.5, bias=lnc[:, 0:1])
            yield
            for j in range(2):
                t = tb + j
                pt, R_pt = bB()
                ptv = bf(pt[:, :])
                PE_tr([(ptv[:, h * 128:(h + 1) * 128], knb[j][:, h, :], 128) for h in range(8)],
                      [R_knb[j], R_const], [R_pt])
                sch.op("dve", lambda e, t=t, ptv=ptv: e.tensor_copy(
                    out=KT[:, :, t * 128:(t + 1) * 128], in_=ptv.rearrange("p (k n) -> p k n", k=8)),
                    [], [R_pt, R_KT[t]])
                yield

        tbs = list(range(0, NT, 2))
        extra = [gen_diag_setup()] if s == 0 else []
        interleave([p1_A(tbs[0])] + extra[:0])
        dgen = extra[0] if extra else None
        for ip, tb in enumerate(tbs):
            gens = [p1_B(tb)]
            if ip + 1 < len(tbs):
                gens.insert(0, p1_A(tbs[ip + 1]))
            if dgen is not None:
                gens.append(dgen)
            interleave_partial = gens
            main = [g_ for g_ in gens if g_ is not dgen]
            while main:
                for g_ in list(gens):
                    try:
                        next(g_)
                    except StopIteration:
                        gens.remove(g_)
                        if g_ in main:
                            main.remove(g_)
                        if g_ is dgen:
                            dgen = None
        if dgen is not None:
            interleave([dgen])

        for g in range(NG):
            t0 = g * 512
            gpool[0] = list(range(8))
            reqs = []
            for c in range(8):
                reqs += [(w_in, O_CA + c * 128, 128, 8), (w_in, O_CB + c * 128, 128, 8),
                         (w_in, O_GC + c * 128, 128, 8)]
            for m in range(8):
                reqs += [(w_pw2, m * 128, 128, 8), (w_in, O_MC + m * 128, 128, 8)]
            ws1 = gws
            reqs2 = [(w_in, O_GA + h * 128, 128, 8) for h in range(8)]
            for m in range(8):
                reqs2 += [(w_o, m * 128, 128, 8), (w_in, O_MA + m * 128, 128, 8)]
            reqs2 += [(w_out, cb * 128, 128, 8) for cb in range(8)]
            ws2 = gws

            def emit_halo_front(gg):
                tg0 = gg * 512
                zr = []
                if gg > 0:
                    zr.append((tg0 - 15, 15, 0))
                if gg < NG - 1:
                    zr.append((tg0 + 512, 15, 15))
                load_h_halo_front(s, zr)

            def lh_specs(gg, lo, hi):
                return [(gg * 512 + tt * 128, hT, tt * 128, R_hT[tt]) for tt in range(lo, hi)]

            def emit_load_h_main(gg):
                load_h_batch(s, lh_specs(gg, 0, 2))
                load_h_batch(s, lh_specs(gg, 2, 4))

            if g == 0:
                emit_halo_front(0)
                load_h_halo_back()
                emit_load_h_main(0)

            def conv_AB(c):
                wa, R_wa = ws1.get()
                wb, R_wb = ws1.get()
                pa, R_pa = gbank()
                PE_mm([(pa[:, :], wa[:, kc, :], hT[:, kc, :], kc == 0, kc == 7) for kc in range(8)],
                      R_hT + [R_wa], [R_pa])
                pbk, R_pbk = gbank()
                PE_mm([(pbk[:, :], wb[:, kc, :], hT[:, kc, :], kc == 0, kc == 7) for kc in range(8)],
                      R_hT + [R_wb], [R_pbk])
                ph, R_ph = gbank()
                PE_mm([(ph[:, 0:30], wa[:, kc, :], hTh[:, kc, 0:30], kc == 0, kc == 7) for kc in range(8)]
                      + [(ph[:, 32:62], wb[:, kc, :], hTh[:, kc, 0:30], kc == 0, kc == 7) for kc in range(8)],
                      [R_hTh, R_wa, R_wb], [R_ph])
                sg, R_sg = ftile()
                ACT(sg[:, :], pbk[:, :], AF.Sigmoid, [], [R_pbk, R_sg])
                DVE_tt(u[:, c, 16:528], pa[:, :], sg[:, :], ALU.mult, [R_sg], [R_pa, R_u[c]] + R_qT + R_alias + R_sma + R_smc)
                sgh, R_sgh = ftile()
                ACT(sgh[:, 0:30], ph[:, 32:62], AF.Sigmoid, [], [R_ph, R_sgh])
                DVE_tt(u[:, c, 1:16], ph[:, 0:15], sgh[:, 0:15], ALU.mult, [R_sgh], [R_ph, R_u[c]] + R_smc)
                DVE_tt(u[:, c, 528:543], ph[:, 15:30], sgh[:, 15:30], ALU.mult, [R_sgh], [R_ph, R_u[c]] + R_smc)

            dg_next = load_diag(0)
            conv_AB(0)
            for c in range(8):
                dg, R_dg = dg_next
                if c + 1 < 8:
                    dg_next = load_diag(c + 1)
                    conv_AB(c + 1)
                pc, R_pc = gbank()
                PE_mm([(pc[:, :], dg[:, k, :], u[:, c, k + 1:k + 513], k == 0, k == KW - 1) for k in range(KW)],
                      [R_dg, R_u[c]], [R_pc])
                ACT(cv[:, c, :], pc[:, :], AF.Identity, [R_const], [R_pc, R_cv[c]] + R_qT + R_alias + R_wu + R_sma,
                    bias=vecs[:, c:c + 1])
            if dbg_stop == "cv":
                stop = True
                break
            pm_, R_pm = gbank()
            pq_, R_pq = gbank()
            for c in range(8):
                sq, R_sq = btile()
                ACT(sq[:, :], cv[:, c, :], AF.Square, [R_cv[c]], [R_sq])
                PE_mm([(pm_[:, :], onesN[:], cv[:, c, :], c == 0, c == 7)], [R_cv[c], R_const], [R_pm])
                PE_mm([(pq_[:, :], onesN[:], sq[:, :], c == 0, c == 7)], [R_sq, R_const], [R_pq])
            mean, R_mean = mean_t, R_mean_t
            ACT_copy(mean[:, :], pm_[:, :], [], [R_pm, R_mean])
            var, R_var = var_t, R_var_t
            DVE_tt(var[:, :], mean[:, :], mean[:, :], ALU.mult, [R_mean], [R_var])
            DVE_tt(var[:, :], pq_[:, :], var[:, :], ALU.subtract, [R_var], [R_pq, R_var])
            for c in range(8):
                wg, R_wg = ws1.get()
                pg, R_pg = gbank()
                PE_mm([(pg[:, :], wg[:, kc, :], hT[:, kc, :], kc == 0, kc == 7) for kc in range(8)],
                      R_hT + [R_wg], [R_pg])
                ACT(mc[:, c, :], pg[:, :], AF.Silu, [], [R_pg, R_mc[c]])
            for m in range(8):
                wm, R_wm = ws1.get()
                pg, R_pg = gbank()
                PE_mm([(pg[:, :], wm[:, kc, :], hT[:, kc, :], kc == 0, kc == 7) for kc in range(8)],
                      R_hT + [R_wm], [R_pg])
                ACT(smc[:, m, :], pg[:, :], AF.Sigmoid, [], [R_pg, R_smc[m]] + R_u)
            DVE_ts(var[:, :], var[:, :], 0.0, None, ALU.max, None, [R_var], [R_var])
            ACT(var[:, :], var[:, :], AF.Ln, [R_const, R_var], [R_var], bias=epst[:, 0:1])
            ACT(var[:, :], var[:, :], AF.Exp, [R_var], [R_var], scale=-0.5)
            DVE_tt(mean[:, :], mean[:, :], var[:, :], ALU.mult, [R_var, R_mean], [R_mean])
            wps = [ws1.get(ahead=0) for m in range(8)]
            for c in range(8):
                tq, R_tq = ftile()
                sch.op("pool" if c % 2 else "dve",
                       lambda e, tq=tq, c=c: e.tensor_tensor(out=tq[:, :], in0=cv[:, c, :], in1=var[:, :],
                                                             op=ALU.mult), [R_cv[c], R_var], [R_tq])
                DVE_tt(tq[:, :], tq[:, :], mean[:, :], ALU.subtract, [R_mean, R_tq], [R_tq])
                ACT(tq[:, :], tq[:, :], AF.Silu, [R_const, R_tq], [R_tq], scale=vecs[:, 8 + c:9 + c],
                    bias=vecs[:, 16 + c:17 + c])
                DVE_tt(cv[:, c, :], tq[:, :], mc[:, c, :], ALU.mult, [R_tq, R_mc[c]], [R_cv[c]])
                for m in range(8):
                    wp, R_wp = wps[m]
                    PE_mm([(banks[m][:, :], wp[:, c, :], cv[:, c, :], c == 0, c == 7)], [R_cv[c], R_wp], [R_bank[m]])
            for m in range(8):
                DVE_tt(mc[:, m, :], banks[m][:, :], smc[:, m, :], ALU.mult, [R_smc[m]], [R_bank[m], R_mc[m]])
            gws.prefetch()

            if dbg_stop == "conv":
                stop = True
                break
            wq = wq_sb
            sch.dma("pool", wq, w_in[:, O_CQ:O_CQ + QR].rearrange("(k p) n -> p k n", p=128), R_wq,
                    writes=[R_wq] + R_knb + R_attn)
            pbs = []
            for tt in range(4):
                pb, R_pb = gbank()
                PE_mm([(pb[:, 0:QR], hT[:, kc, tt * 128:(tt + 1) * 128], wq[:, kc, :], kc == 0, kc == 7)
                       for kc in range(8)], [R_hT[tt], R_wq] + R_attn, [R_pb])
                pbs.append((pb, R_pb))
            for tt in range(4):
                pb, R_pb = pbs[tt]
                jk, R_jk = jslot(QR)
                ACT(jk, pb[:, 0:QR], AF.Square, [], [R_pb, R_stA[tt]] + R_jk, accum_out=stA[:, tt:tt + 1])
            rstd_inplace(stA[:, 0:4], 1.0 / QR, R_stA)
            for tt in range(4):
                pb, R_pb = pbs[tt]
                DVE_stt(nrm[tt][:, :], pb[:, 0:QR], stA[:, tt:tt + 1], qlg[:], ALU.mult, ALU.mult,
                        [R_stA[tt], R_const], [R_pb, R_nrm[tt]])
            for tt in range(4):
                pt, R_pt = gbank()
                ptv = bf(pt[:, :])
                PE_tr([(ptv[:, kc * 128:(kc + 1) * 128], nrm[tt][:, kc * 128:(kc + 1) * 128], 128)
                       for kc in range(3)], [R_nrm[tt], R_const], [R_pt])
                ACT_copy(cqT[:, :, tt * 128:(tt + 1) * 128], ptv[:, 0:384].rearrange("p (k n) -> p k n", k=3),
                         [], [R_pt, R_cqT[tt]])
            csl = [load_cs(g * 4 + tt) for tt in range(4)]
            def q_front(hp):
                wu, R_wuc = wu_sb[hp % 2], R_wu[hp % 2]
                sch.dma("pool", wu, w_uq[:, hp * 384:(hp + 1) * 384].rearrange("(k p) n -> p k n", p=128), R_wuc,
                        writes=[R_wuc] + R_cv + R_alias + R_sma)
                qb, R_qb = qb2[hp % 2], R_qb2[hp % 2]
                sB = stB[:, (hp % 2) * 8:(hp % 2) * 8 + 8]
                R_sB = R_stB[hp % 2]
                pbs = []
                for tt in range(4):
                    pb, R_pb = gbank()
                    PE_mm([(pb[:, 0:384], cqT[:, kc, tt * 128:(tt + 1) * 128], wu[:, kc, :], kc == 0, kc == 2)
                           for kc in range(3)], [R_cqT[tt], R_wuc], [R_pb])
                    pbs.append((pb, R_pb))
                for tt in range(4):
                    pb, R_pb = pbs[tt]
                    pbv = pb[:, 0:384].rearrange("p (j d) -> p j d", j=2)
                    for j in range(2):
                        jk, R_jk = jslot(DH)
                        ACT(jk, pbv[:, j, :], AF.Square, [], [R_pb, R_sB[tt]] + R_jk,
                            accum_out=sB[:, tt * 2 + j:tt * 2 + j + 1])
                rstd_inplace(sB[:, 0:8], 1.0 / DH, R_sB)
                for tt in range(4):
                    pb, R_pb = pbs[tt]
                    pbv = pb[:, 0:384].rearrange("p (j d) -> p j d", j=2)
                    rb = sB[:, tt * 2:tt * 2 + 2].unsqueeze(2).broadcast_to([128, 2, 128])
                    DVE_tt(qb[tt][:, 0:256].rearrange("p (j d) -> p j d", j=2), pbv[:, :, 0:128], rb, ALU.mult,
                           [R_sB[tt]], [R_pb, R_qb[tt]])
                    for j in range(2):
                        sc_ = sB[:, tt * 2 + j:tt * 2 + j + 1]
                        DVE_stt(qrf[tt][:, 0, j, :], pbv[:, j, 128:192], sc_, gq[:, 128:192],
                                ALU.mult, ALU.mult, [R_sB[tt], R_const], [R_pb, R_qrf[tt]])
                for tt in range(4):
                    cst, R_cs = csl[tt]
                    q0 = qrf[tt]
                    s1b = cst[:, 64:96].unsqueeze(1).broadcast_to([128, 2, 32])
                    s2b = cst[:, 96:128].unsqueeze(1).broadcast_to([128, 2, 32])
                    csb = cst[:, 0:64].unsqueeze(1).broadcast_to([128, 2, 64])
                    DVE_tt(q0[:, 1, :, 0:32], q0[:, 0, :, 32:64], s1b, ALU.mult, [R_cs, R_qrf[tt]], [R_qrf[tt]])
                    DVE_tt(q0[:, 1, :, 32:64], q0[:, 0, :, 0:32], s2b, ALU.mult, [R_cs, R_qrf[tt]], [R_qrf[tt]])
                    DVE_tt(q0[:, 0, :, :], q0[:, 0, :, :], csb, ALU.mult, [R_cs, R_qrf[tt]], [R_qrf[tt]])
                    DVE_tt(qb[tt][:, 256:384].rearrange("p (j d) -> p j d", j=2), q0[:, 0, :, :], q0[:, 1, :, :],
                           ALU.add, [R_qrf[tt]], [R_qb[tt]])

            def q_back(hp):
                qb, R_qb = qb2[hp % 2], R_qb2[hp % 2]
                for tt in range(4):
                    pt, R_pt = gbank()
                    ptv = bf(pt[:, :])
                    PE_tr([(ptv[:, 0:128], qb[tt][:, 0:128], 128), (ptv[:, 128:256], qb[tt][:, 128:256], 128),
                           (ptv[:, 256:384], qb[tt][:, 256:384], 128)], [R_qb[tt], R_const], [R_pt])
                    ACT_copy(qT[:, 2 * hp:2 * hp + 2, tt * 128:(tt + 1) * 128],
                             ptv[:, 0:256].rearrange("p (k n) -> p k n", k=2),
                             [], [R_pt, R_qT[tt]] + R_u + R_cv + R_alias + R_sma + R_smc)
                    ACT_copy(qT[:, 8 + hp, tt * 128:(tt + 1) * 128], ptv[:, 256:384],
                             [], [R_pt, R_qT[tt]] + R_u + R_cv + R_alias + R_sma + R_smc)

            q_front(0)
            for hp in range(4):
                if hp + 1 < 4:
                    q_front(hp + 1)
                q_back(hp)

            gpool[0] = [7]
            R_qTall = R_qT
            SB = [0, 1, 6]
            LOOK = 2
            its = [(h, kt) for h in range(8) for kt in range(NT)]

            def emit_S(i):
                h, kt = its[i]
                Sb, R_Sb = banks[SB[i % 3]], R_bank[SB[i % 3]]
                ks = slice(kt * 128, (kt + 1) * 128)
                krX = krTa if h % 2 == 0 else krTb
                PE_mm([(Sb[:, :], KT[:, h, ks], qT[:, h, :], True, False),
                       (Sb[:, :], krX[:, ks], qT[:, 8 + h // 2, :], False, True)],
                      [R_KT[kt]] + R_qTall, [R_Sb])
                P_, R_P = Pb[i % NPB], R_Pb[i % NPB]
                ACT(P_[:, :], Sb[:, :], AF.Exp, [R_KT[kt], R_const], [R_Sb, R_P], scale=sck[:, kt, h:h + 1],
                    bias=negB[:, 0:1])

            for i in range(min(LOOK, len(its))):
                emit_S(i)
            for i, (h, kt) in enumerate(its):
                if i + LOOK < len(its):
                    emit_S(i + LOOK)
                Ob, R_Ob = banks[2 + h % 2], R_bank[2 + h % 2]
                Rb, R_Rb = banks[4 + h % 2], R_bank[4 + h % 2]
                P_, R_P = Pb[i % NPB], R_Pb[i % NPB]
                PE_mm([(Ob[:, :], V[:, kt, h * 128:(h + 1) * 128], P_[:, :], kt == 0, kt == NT - 1)],
                      [R_KT[kt], R_P], [R_Ob])
                PE_mm([(Rb[:, :], ones_b[:], P_[:, :], kt == 0, kt == NT - 1)], [R_P, R_const], [R_Rb])
                if kt == NT - 1:
                    rc, R_rc = ftile()
                    sch.op("dve", lambda e, rc=rc, Rb=Rb: e.reciprocal(out=rc[:, :], in_=Rb[:, :]),
                           [], [R_Rb, R_rc])
                    DVE_tt(attn[:, h, :], Ob[:, :], rc[:, :], ALU.mult, [R_rc],
                           [R_Ob, R_attn[h], R_wq] + R_knb)
            gpool[0] = list(range(8))

            if g + 1 < NG:
                emit_halo_front(g + 1)
            for h in range(8):
                wga, R_wga = ws2.get()
                pg, R_pg = gbank()
                PE_mm([(pg[:, :], wga[:, kc, :], hT[:, kc, :], kc == 0, kc == 7) for kc in range(8)],
                      R_hT + [R_wga], [R_pg])
                sg, R_sg = ftile()
                ACT(sg[:, :], pg[:, :], AF.Silu, [], [R_pg, R_sg])
                DVE_tt(attn[:, h, :], attn[:, h, :], sg[:, :], ALU.mult, [R_sg, R_attn[h]], [R_attn[h]])
            for m in range(8):
                wm, R_wm = ws2.get()
                pg, R_pg = gbank()
                PE_mm([(pg[:, :], wm[:, kc, :], hT[:, kc, :], kc == 0, kc == 7) for kc in range(8)],
                      R_hT + [R_wm], [R_pg])
                ACT(smaF[:, m * 512:(m + 1) * 512], pg[:, :], AF.Sigmoid, [],
                    [R_pg, R_sma[m]] + R_u + R_cv + R_qT + R_wu + R_alias + R_smc)
            lh1 = None
            if g + 1 < NG:
                load_h_halo_back()
                lh1 = load_h_front(s, lh_specs(g + 1, 0, 2))
            for m in range(8):
                wo, R_wo = ws2.get()
                py, R_py = gbank()
                PE_mm([(py[:, :], wo[:, kc, :], attn[:, kc, :], kc == 0, kc == 7) for kc in range(8)],
                      R_attn + [R_wo], [R_py])
                sg, R_sg = ftile()
                DVE_tt(sg[:, :], py[:, :], smaF[:, m * 512:(m + 1) * 512], ALU.mult, [R_sma[m]], [R_py, R_sg])
                DVE_tt(mc[:, m, :], mc[:, m, :], sg[:, :], ALU.add, [R_sg, R_mc[m]], [R_mc[m]])
            lh2 = None
            if lh1 is not None:
                load_h_back(lh1)
                lh2 = load_h_front(s, lh_specs(g + 1, 2, 4))
            wots = [ws2.get(ahead=0) for cb in range(8)]
            for tt in range(4):
                i = xctr[0] % 2
                xctr[0] += 1
                xt, R_x = xts[i], R_xt[i]
                r0 = t0 + tt * 128
                sch.dma("sp", xt[:, :], x[s, r0:r0 + 128, :], R_x, writes=[R_x])
                for cb in range(8):
                    wo_t, R_wot = wots[cb]
                    bi = 2 * tt + cb // 4
                    co = (cb % 4) * 128
                    PE_mm([(banks[bi][:, co:co + 128], mc[:, kc, tt * 128:(tt + 1) * 128], wo_t[:, kc, :],
                            kc == 0, kc == 7) for kc in range(8)], R_mc + [R_wot], [R_bank[bi]])
                for hf in range(2):
                    bi = 2 * tt + hf
                    DVE_tt(xt[:, hf * 512:(hf + 1) * 512], xt[:, hf * 512:(hf + 1) * 512], banks[bi][:, :], ALU.add,
                           [R_x], [R_bank[bi], R_x])
                sch.dma("sp", y[s, r0:r0 + 128, :], xt[:, :], R_x, reads=[R_x])
            gws.prefetch()
            if lh2 is not None:
                load_h_back(lh2)

    sch.final_wait("sp", R_xt)

    with nc.Block() as block:
        @block.sync
        def _(e):
            for f in sch.prog["sp"]:
                f(e)

        @block.vector
        def _(e):
            for f in sch.prog["dve"]:
                f(e)

        @block.tensor
        def _(e):
            for f in sch.prog["pe"]:
                f(e)

        @block.scalar
        def _(e):
            for f in sch.prog["act"]:
                f(e)

        @block.gpsimd
        def _(e):
            for f in sch.prog["pool"]:
                f(e)
    build_nc.sbuf_left = nc.sbuf_bytes_remaining
    return nc


def host_consts(S):
    half = 32
    inv_freq = (1.0 / (np.float32(10000.0) ** (np.arange(half, dtype=np.float32) / np.float32(half)))).astype(np.float32)
    ang = (np.arange(S, dtype=np.float32)[:, None] * inv_freq[None, :]).astype(np.float32)
    cos = np.cos(ang).astype(np.float32)
    sin = np.sin(ang).astype(np.float32)
    return np.ascontiguousarray(np.concatenate([cos, cos, -sin, sin], axis=1).astype(np.float32))


def make_in_maps(xs, S, norm_g, w_in, q_lora_g, w_uq, kv_lora_g, w_ukv, q_head_g, k_head_g, w_o_attn,
                 dw_kernel, dw_bias, conv_ln_g, conv_ln_b, w_pw2, w_out):
    f = lambda a: np.ascontiguousarray(np.asarray(a, dtype=np.float32))
    csss = host_consts(S)
    dwk = f(dw_kernel[0]).T.reshape(8, 128, KW).transpose(1, 0, 2).reshape(128, 8 * KW)
    vec = lambda v: f(v[0]).reshape(8, 128).T
    vecs = np.concatenate([vec(dw_bias), vec(conv_ln_g), vec(conv_ln_b)], axis=1)
    common = {
        "w_in": f(w_in[0]), "w_uq": f(w_uq[0]), "w_ukv": f(w_ukv[0]), "w_o": f(w_o_attn[0]),
        "w_pw2": f(w_pw2[0]), "w_out": f(w_out[0]), "ident": np.eye(128, dtype=np.float32),
        "norm_g": f(norm_g), "q_lora_g": f(q_lora_g), "kv_lora_g": f(kv_lora_g),
        "q_head_g": f(q_head_g), "k_head_g": f(k_head_g), "csss": csss,
        "dwk": f(dwk), "vecs": f(vecs),
    }
    return [dict(common, x=f(xc)) for xc in xs]


def kernel(x_prompt, x_sample, norm_g, w_in, q_lora_g, w_uq, kv_lora_g, w_ukv, q_head_g, k_head_g,
           w_o_attn, dw_kernel, dw_bias, conv_ln_g, conv_ln_b, w_pw2, w_out):
    x_prompt = np.asarray(x_prompt, dtype=np.float32)
    x_sample = np.asarray(x_sample, dtype=np.float32)
    S = x_prompt.shape[1]
    xs = [np.concatenate([x_prompt[c:c + 1], x_sample[2 * c:2 * c + 2]], axis=0) for c in range(NCORES)]
    nc = build_nc(S, 3)
    in_maps = make_in_maps(xs, S, norm_g, w_in, q_lora_g, w_uq, kv_lora_g, w_ukv, q_head_g, k_head_g,
                           w_o_attn, dw_kernel, dw_bias, conv_ln_g, conv_ln_b, w_pw2, w_out)
    res = run_bass_kernel_spmd(nc, in_maps, core_ids=list(range(NCORES)))
    ys = [np.asarray(r["y"], dtype=np.float32) for r in res.results]
    y_prompt = np.stack([ys[c][0] for c in range(NCORES)], axis=0)
    y_sample = np.stack([ys[c][1 + j] for c in range(NCORES) for j in range(2)], axis=0)
    return (y_prompt, y_sample)
```
